# Optimizing a Trainium2 kernel written in Bass

```python
import math
import jax, jax.numpy as jnp
from jax import lax
import numpy as np

D_MODEL = 1024
BATCH = 2
SEQ = 8192
DEPTH = 2
DEC_BATCH = 8
DEC_SEQ = 8192
PAST_LEN = 128

N_META = 16
BLOCK = 128
META_START = BLOCK - N_META
D_MIX = D_MODEL
A_HEADS = 4
A_QK_DIM = 64
A_V_DIM = 2 * A_QK_DIM
B_HEADS = 4
B_KV_HEADS = 2
B_HEAD_DIM = 64
WINDOW = 128
C_HEADS = 4
C_Q_RANK = 256
C_KV_RANK = 128
C_NOPE_DIM = 64
C_ROPE_DIM = 32
C_V_DIM = 64
ROPE_THETA = 10000.0
N_BUCKETS = 32
MAX_DISTANCE = 128
N_BIAS_HEADS = A_HEADS + B_HEADS
D_FF = 2816
EPS = 1e-6

IN_SIZES = (A_HEADS * 2 * A_QK_DIM, A_HEADS * 2 * A_QK_DIM, A_HEADS * A_V_DIM,
            B_HEADS * B_HEAD_DIM, B_KV_HEADS * B_HEAD_DIM, B_KV_HEADS * B_HEAD_DIM,
            C_Q_RANK, C_KV_RANK, C_ROPE_DIM)
D_IN = sum(IN_SIZES)
SPLIT_POINTS = tuple(sum(IN_SIZES[:i + 1]) for i in range(len(IN_SIZES) - 1))

kernel_name = "hymba_style_diff_window_mla_encoder"


def rmsnorm(x, g):
    xf = x.astype(jnp.float32)
    y = xf * lax.rsqrt(jnp.mean(xf * xf, axis=-1, keepdims=True) + EPS)
    return (y * g.astype(jnp.float32)).astype(x.dtype)


def swiglu(x, w_gu, w_down):
    g, u = jnp.split(x @ w_gu, 2, axis=-1)
    return (jax.nn.silu(g) * u) @ w_down


def t5_bucket(rel):
    half = N_BUCKETS // 2
    max_exact = half // 2
    ret = jnp.where(rel > 0, half, 0)
    n = jnp.abs(rel)
    nf = jnp.maximum(n, 1).astype(jnp.float32)
    large = max_exact + (jnp.log(nf / max_exact) / math.log(MAX_DISTANCE / max_exact)
                         * (half - max_exact)).astype(jnp.int32)
    large = jnp.minimum(large, half - 1)
    return ret + jnp.where(n < max_exact, n, large)


def rope_tables(pos):
    inv = ROPE_THETA ** (-jnp.arange(0, C_ROPE_DIM, 2, dtype=jnp.float32) / C_ROPE_DIM)
    ang = pos[:, None] * inv[None, :]
    ang = jnp.concatenate([ang, ang], axis=-1)
    return jnp.cos(ang), jnp.sin(ang)


def apply_rope(x, cos, sin):
    x1, x2 = jnp.split(x, 2, axis=-1)
    rot = jnp.concatenate([-x2, x1], axis=-1)
    return (x.astype(jnp.float32) * cos + rot.astype(jnp.float32) * sin).astype(x.dtype)


def diff_attention(q, k, v, lam, bias_tab, key_ok, slot):
    B, Lp = q.shape[0], q.shape[1]
    nb = Lp // BLOCK
    qb = jnp.moveaxis(q.reshape(B, nb, BLOCK, A_HEADS, 2, A_QK_DIM), 1, 0)
    scale = A_QK_DIM ** -0.5

    def one_block(args):
        qi, start = args
        s = jnp.einsum('bqhmd,bkhmd->bhmqk', qi, k).astype(jnp.float32) * scale
        qslot = start + jnp.arange(BLOCK)
        bias = bias_tab[t5_bucket(slot[None, :] - qslot[:, None])].astype(jnp.float32)
        s = s + jnp.transpose(bias, (2, 0, 1))[None, :, None]
        s = jnp.where(key_ok, s, -jnp.inf)
        p = jax.nn.softmax(s, axis=-1)
        w = p[:, :, 0] - lam * p[:, :, 1]
        return jnp.einsum('bhqk,bkhe->bqhe', w.astype(v.dtype), v)

    o = lax.map(one_block, (qb, jnp.arange(nb) * BLOCK))
    return jnp.moveaxis(o, 0, 1).reshape(B, Lp, A_HEADS, A_V_DIM)


def window_gqa(q, k, v, sinks, bias_tab):
    B, Lp = q.shape[0], q.shape[1]
    nb = Lp // BLOCK
    G = B_HEADS // B_KV_HEADS
    qb = q.reshape(B, nb, BLOCK, B_KV_HEADS, G, B_HEAD_DIM)

    def neighbours(t):
        tb = t.reshape(B, nb, BLOCK, B_KV_HEADS, B_HEAD_DIM)
        tp = jnp.pad(tb, ((0, 0), (1, 1), (0, 0), (0, 0), (0, 0)))
        return jnp.concatenate([tp[:, :-2], tp[:, 1:-1], tp[:, 2:]], axis=2)

    kn, vn = neighbours(k), neighbours(v)
    s = jnp.einsum('bnqhgd,bnkhd->bnhgqk', qb, kn).astype(jnp.float32) * B_HEAD_DIM ** -0.5
    rel = (jnp.arange(3 * BLOCK) - BLOCK)[None, :] - jnp.arange(BLOCK)[:, None]
    bias = bias_tab[t5_bucket(rel)].astype(jnp.float32).reshape(BLOCK, 3 * BLOCK, B_KV_HEADS, G)
    s = s + jnp.transpose(bias, (2, 3, 0, 1))[None, None]
    kslot = (jnp.arange(nb)[:, None] - 1) * BLOCK + jnp.arange(3 * BLOCK)[None, :]
    ok = (jnp.abs(rel) <= WINDOW)[None] & ((kslot >= META_START) & (kslot < Lp))[:, None, :]
    s = jnp.where(ok[None, :, None, None], s, -jnp.inf)
    sink = jnp.broadcast_to(sinks.astype(jnp.float32).reshape(B_KV_HEADS, G, 1, 1), s.shape[:-1] + (1,))
    p = jax.nn.softmax(jnp.concatenate([s, sink], axis=-1), axis=-1)[..., :-1]
    o = jnp.einsum('bnhgqk,bnkhd->bnqhgd', p.astype(v.dtype), vn)
    return o.reshape(B, Lp, B_HEADS * B_HEAD_DIM)


def mla(cq, ckv, kr, g_cq, g_ckv, w_uq, w_ukv, cos, sin, key_ok):
    B, Lp = cq.shape[0], cq.shape[1]
    nb = Lp // BLOCK
    q = (rmsnorm(cq, g_cq) @ w_uq).reshape(B, Lp, C_HEADS, C_NOPE_DIM + C_ROPE_DIM)
    q = jnp.concatenate([q[..., :C_NOPE_DIM],
                         apply_rope(q[..., C_NOPE_DIM:], cos[:, None], sin[:, None])], axis=-1)
    kv = (rmsnorm(ckv, g_ckv) @ w_ukv).reshape(B, Lp, C_HEADS, C_NOPE_DIM + C_V_DIM)
    k_nope, v = kv[..., :C_NOPE_DIM], kv[..., C_NOPE_DIM:]
    k_rope = apply_rope(kr, cos, sin)
    k = jnp.concatenate([k_nope, jnp.broadcast_to(k_rope[:, :, None], (B, Lp, C_HEADS, C_ROPE_DIM))], axis=-1)
    scale = (C_NOPE_DIM + C_ROPE_DIM) ** -0.5
    qb = jnp.moveaxis(q.reshape(B, nb, BLOCK, C_HEADS, C_NOPE_DIM + C_ROPE_DIM), 1, 0)

    def one_block(qi):
        s = jnp.einsum('bqhd,bkhd->bhqk', qi, k).astype(jnp.float32) * scale
        s = jnp.where(key_ok, s, -jnp.inf)
        p = jax.nn.softmax(s, axis=-1)
        return jnp.einsum('bhqk,bkhd->bqhd', p.astype(v.dtype), v)

    o = lax.map(one_block, qb)
    return jnp.moveaxis(o, 0, 1).reshape(B, Lp, C_HEADS * C_V_DIM)


def trunk(x, meta, rel_bias, g_ffn1, w_ffn1_gu, w_ffn1_down, g_mix, w_in, diff_lambda, g_subln,
          sinks, g_cq, g_ckv, w_uq, w_ukv, w_out, g_ffn2, w_ffn2_gu, w_ffn2_down, g_final):
    B, S, D = x.shape
    Lp = BLOCK + S
    lead = jnp.concatenate([jnp.zeros((META_START, D), x.dtype), meta.astype(x.dtype)], axis=0)
    h = jnp.concatenate([jnp.broadcast_to(lead[None], (B, BLOCK, D)), x], axis=1)
    slot = jnp.arange(Lp)
    key_ok = slot >= META_START
    cos, sin = rope_tables((slot - META_START).astype(jnp.float32))
    for l in range(DEPTH):
        h = h + 0.5 * swiglu(rmsnorm(h, g_ffn1[l]), w_ffn1_gu[l], w_ffn1_down[l])
        u = rmsnorm(h, g_mix[l]) @ w_in[l]
        aq, ak, av, bq, bk, bv, cq, ckv, kr = jnp.split(u, SPLIT_POINTS, axis=-1)
        lam_init = 0.8 - 0.6 * math.exp(-0.3 * l)
        dl = diff_lambda[l].astype(jnp.float32)
        lam = jnp.exp(jnp.sum(dl[0] * dl[1])) - jnp.exp(jnp.sum(dl[2] * dl[3])) + lam_init
        oa = diff_attention(aq.reshape(B, Lp, A_HEADS, 2, A_QK_DIM), ak.reshape(B, Lp, A_HEADS, 2, A_QK_DIM),
                            av.reshape(B, Lp, A_HEADS, A_V_DIM), lam, rel_bias[:, :A_HEADS], key_ok, slot)
        oa = (rmsnorm(oa, g_subln[l]) * (1.0 - lam_init)).reshape(B, Lp, A_HEADS * A_V_DIM)
        ob = window_gqa(bq.reshape(B, Lp, B_HEADS, B_HEAD_DIM), bk.reshape(B, Lp, B_KV_HEADS, B_HEAD_DIM),
                        bv.reshape(B, Lp, B_KV_HEADS, B_HEAD_DIM), sinks[l], rel_bias[:, A_HEADS:])
        oc = mla(cq, ckv, kr, g_cq[l], g_ckv[l], w_uq[l], w_ukv[l], cos, sin, key_ok)
        h = h + jnp.concatenate([oa, ob, oc], axis=-1) @ w_out[l]
        h = h + 0.5 * swiglu(rmsnorm(h, g_ffn2[l]), w_ffn2_gu[l], w_ffn2_down[l])
    return rmsnorm(h[:, BLOCK:], g_final)


def setup_inputs(seed: int = 0) -> dict:
    key = jax.random.key(seed)
    ks = jax.random.split(key, 24)

    def nrm(k, shape, scale):
        return jax.random.normal(k, shape, jnp.float32) * scale

    def gain(k, shape):
        return 1.0 + 0.05 * jax.random.normal(k, shape, jnp.float32)

    return {
        "x_prompt": nrm(ks[0], (BATCH, SEQ, D_MODEL), 1.0),
        "x_sample": nrm(ks[1], (DEC_BATCH, DEC_SEQ, D_MODEL), 1.0),
        "meta": nrm(ks[2], (N_META, D_MODEL), 1.0),
        "rel_bias": nrm(ks[3], (N_BUCKETS, N_BIAS_HEADS), 0.5),
        "g_ffn1": gain(ks[4], (DEPTH, D_MODEL)),
        "w_ffn1_gu": nrm(ks[5], (DEPTH, D_MODEL, 2 * D_FF), D_MODEL ** -0.5),
        "w_ffn1_down": nrm(ks[6], (DEPTH, D_FF, D_MODEL), D_FF ** -0.5),
        "g_mix": gain(ks[7], (DEPTH, D_MODEL)),
        "w_in": nrm(ks[8], (DEPTH, D_MODEL, D_IN), D_MODEL ** -0.5),
        "diff_lambda": nrm(ks[9], (DEPTH, 4, A_QK_DIM), 0.1),
        "g_subln": gain(ks[10], (DEPTH, A_V_DIM)),
        "sinks": nrm(ks[11], (DEPTH, B_HEADS), 0.5),
        "g_cq": gain(ks[12], (DEPTH, C_Q_RANK)),
        "g_ckv": gain(ks[13], (DEPTH, C_KV_RANK)),
        "w_uq": nrm(ks[14], (DEPTH, C_Q_RANK, C_HEADS * (C_NOPE_DIM + C_ROPE_DIM)), C_Q_RANK ** -0.5),
        "w_ukv": nrm(ks[15], (DEPTH, C_KV_RANK, C_HEADS * (C_NOPE_DIM + C_V_DIM)), C_KV_RANK ** -0.5),
        "w_out": nrm(ks[16], (DEPTH, D_MIX, D_MODEL), D_MIX ** -0.5),
        "g_ffn2": gain(ks[17], (DEPTH, D_MODEL)),
        "w_ffn2_gu": nrm(ks[18], (DEPTH, D_MODEL, 2 * D_FF), D_MODEL ** -0.5),
        "w_ffn2_down": nrm(ks[19], (DEPTH, D_FF, D_MODEL), D_FF ** -0.5),
        "g_final": gain(ks[20], (D_MODEL,)),
    }


def reference(x_prompt, x_sample, meta, rel_bias, g_ffn1, w_ffn1_gu, w_ffn1_down, g_mix, w_in,
              diff_lambda, g_subln, sinks, g_cq, g_ckv, w_uq, w_ukv, w_out, g_ffn2, w_ffn2_gu,
              w_ffn2_down, g_final):
    y_prompt = trunk(x_prompt, meta, rel_bias, g_ffn1, w_ffn1_gu, w_ffn1_down, g_mix, w_in, diff_lambda,
                     g_subln, sinks, g_cq, g_ckv, w_uq, w_ukv, w_out, g_ffn2, w_ffn2_gu, w_ffn2_down, g_final)
    y_sample = trunk(x_sample, meta, rel_bias, g_ffn1, w_ffn1_gu, w_ffn1_down, g_mix, w_in, diff_lambda,
                     g_subln, sinks, g_cq, g_ckv, w_uq, w_ukv, w_out, g_ffn2, w_ffn2_gu, w_ffn2_down, g_final)
    return (y_prompt, y_sample)
```

```python
import contextlib
import math
import numpy as np
import concourse.bass as bass
import concourse.mybir as mybir
from concourse.bass_utils import run_bass_kernel_spmd

F32 = mybir.dt.float32
BF16 = mybir.dt.bfloat16
ALU = mybir.AluOpType
AF = mybir.ActivationFunctionType
AX = mybir.AxisListType

D = 1024
KD = 8
DFF = 2816
NFC = 22
NMETA = 16
MS = 112
NEG = -30000.0
EPS = 1e-6
DEPTH = 2
SAME_SYNC = True
FULL_SYNC = False
NQS = 20


class Buf:
    __slots__ = ("w", "r")

    def __init__(self):
        self.w = {}
        self.r = {}


class Tl:
    def __init__(self, t):
        self.t = t
        self.b = Buf()

    def __getitem__(self, idx):
        return self.t[idx]


class Al:
    def __init__(self, fn):
        self.fn = fn
        self.b = Buf()

    def __getitem__(self, idx):
        return self.fn()[idx]


def alias_begin(parent, children):
    for c in children:
        for k, tv in list(parent.b.w.items()) + list(parent.b.r.items()):
            for d_ in (c.b.w,):
                if k not in d_ or d_[k][1] < tv[1]:
                    d_[k] = tv


def alias_end(parent, children):
    for c in children:
        for k, tv in list(c.b.w.items()) + list(c.b.r.items()):
            if k not in parent.b.r or parent.b.r[k][1] < tv[1]:
                parent.b.r[k] = tv


class Eng:
    def __init__(self, h, name, sem, is_pe=False):
        self.h = h
        self.name = name
        self.key = name
        self.sem = sem
        self.n = 0
        self.seen = {}
        self.is_pe = is_pe
        self.qsems = None
        self.qi = 0


class Prog:
    def __init__(self, nc, es):
        self.nc = nc
        self.es = es
        mk = lambda nm: nc.alloc_semaphore(nm)
        self.pe = Eng(nc.tensor, "pe", mk("c_pe"), True)
        self.act = Eng(nc.scalar, "act", mk("c_act"))
        self.dve = Eng(nc.vector, "dve", mk("c_dve"))
        self.pool = Eng(nc.gpsimd, "pool", mk("c_pool"))
        self.sp = Eng(nc.sync, "sp", mk("c_sp"))
        self.engs = [self.pe, self.act, self.dve, self.pool, self.sp]
        for e in (self.sp, self.pool, self.act):
            e.qsems = [mk(f"q_{e.name}_{i}") for i in range(NQS)]

    def _waits(self, E, need):
        for key, (sem, val) in need.items():
            if E.seen.get(key, 0) >= val:
                continue
            E.h.wait_ge(sem, val)
            E.seen[key] = val

    def _need(self, E, r, w):
        need = {}

        def merge(d, same_ok):
            for k, tv in d.items():
                if k == E.key and (E.is_pe or not SAME_SYNC or not (same_ok or FULL_SYNC)):
                    continue
                if k not in need or need[k][1] < tv[1]:
                    need[k] = tv
        for b in r:
            merge(b.w, True)
        for b in w:
            merge(b.w, False)
            merge(b.r, False)
        return need

    def _upd(self, key, tok, r, w):
        for b in r:
            b.r[key] = tok
        for b in w:
            b.w = {key: tok}
            b.r = {}

    def op(self, E, fn, r=(), w=()):
        r = [x if isinstance(x, Buf) else x.b for x in r]
        w = [x if isinstance(x, Buf) else x.b for x in w]
        self._waits(E, self._need(E, r, w))
        ins = fn()
        E.n += 1
        ins.then_inc(E.sem, 1)
        self._upd(E.key, (E.sem, E.n), r, w)

    def dma(self, E, out, in_, r=(), w=(), **kw):
        r = [x if isinstance(x, Buf) else x.b for x in r]
        w = [x if isinstance(x, Buf) else x.b for x in w]
        slot = E.qi % NQS
        rnd = E.qi // NQS
        E.qi += 1
        sem = E.qsems[slot]
        key = f"q_{E.name}_{slot}"
        need = self._need(E, r, w)
        if rnd > 0:
            need[key] = (sem, 16 * rnd)
        self._waits(E, need)
        E.h.dma_start(out=out, in_=in_, **kw).then_inc(sem, 16)
        self._upd(key, (sem, 16 * (rnd + 1)), r, w)

    def barrier(self):
        for E in self.engs:
            need = {}
            for X in self.engs:
                if X is not E and X.n > 0:
                    need[X.key] = (X.sem, X.n)
                if X.qsems is not None:
                    for slot in range(min(NQS, X.qi)):
                        cnt = (X.qi - 1 - slot) // NQS + 1
                        need[f"q_{X.name}_{slot}"] = (X.qsems[slot], 16 * cnt)
            self._waits(E, need)

    def sb(self, name, shape, dt):
        return Tl(self.es.enter_context(self.nc.sbuf_tensor(name, list(shape), dt)))

    def ps(self, name, shape, dt=F32):
        return Tl(self.es.enter_context(self.nc.psum_tensor(name, list(shape), dt)))


class Ring:
    def __init__(self, P, name, shape, dt, depth):
        self.P = P
        self.slots = [P.sb(f"{name}{i}", shape, dt) for i in range(depth)]
        self.depth = depth
        self.plan = []
        self.issued = 0
        self.consumed = 0

    def pop(self):
        lim = min(self.consumed + self.depth, len(self.plan))
        while self.issued < lim:
            self.plan[self.issued](self.slots[self.issued % self.depth])
            self.issued += 1
        s = self.slots[self.consumed % self.depth]
        self.consumed += 1
        return s


def t5_bucket_np(rel):
    rel = np.asarray(rel, np.int32)
    half = 16
    max_exact = 8
    ret = np.where(rel > 0, half, 0)
    n = np.abs(rel)
    nf = np.maximum(n, 1).astype(np.float32)
    try:
        import jax
        import jax.numpy as jnp
        with jax.default_device(jax.devices("cpu")[0]):
            lg = (jnp.log(jnp.asarray(nf) / max_exact) / math.log(128 / max_exact) * (half - max_exact)).astype(jnp.int32)
            lg = np.asarray(lg)
    except Exception:
        lg = (np.log(nf / np.float32(max_exact)) / np.float32(math.log(128 / max_exact)) * np.float32(half - max_exact)).astype(np.int32)
    large = np.minimum(max_exact + lg, half - 1)
    return (ret + np.where(n < max_exact, n, large)).astype(np.int32)


def host_consts(Lp):
    k = np.arange(128)[:, None]
    c = np.arange(384)[None, :]
    d = k - c + 128
    bk = t5_bucket_np(d).astype(np.float32)
    mb = (np.abs(d) <= 128).astype(np.float32)
    negb = np.where(np.abs(d) <= 128, 0.0, NEG).astype(np.float32)
    pos = (np.arange(Lp) - MS).astype(np.float32)
    inv = (10000.0 ** (-np.arange(0, 32, 2, dtype=np.float32) / np.float32(32))).astype(np.float32)
    ang = pos[:, None] * inv[None, :]
    ang = np.concatenate([ang, ang], axis=-1)
    cos = np.cos(ang).astype(np.float32).T.copy()
    sin = np.sin(ang).astype(np.float32).T.copy()
    ss = sin.copy()
    ss[:16] = -ss[:16]
    ident = np.eye(128, dtype=np.float32)
    return {"c_bk": bk, "c_mb": mb, "c_negb": negb, "c_cos": cos, "c_ss": ss, "c_ident": ident}


AQ, AK, AV, BQ, BK_, BV, CQ, CKV, KR = 0, 512, 1024, 1536, 1792, 1920, 2048, 2304, 2432
WIN_SLOTS = []
for _h in range(0, 4, 2):
    WIN_SLOTS.append([(0, AQ + _h * 128, 256)])
for _h in range(0, 4, 2):
    WIN_SLOTS.append([(0, AK + _h * 128, 256)])
WIN_SLOTS.append([(0, BQ, 256)])
WIN_SLOTS.append([(0, BK_, 64), (64, BK_, 64), (128, BK_ + 64, 64), (192, BK_ + 64, 64)])
WIN_SLOTS.append([(0, CQ, 256)])
WIN_SLOTS.append([(0, CKV, 128), (128, CKV, 64), (192, KR, 32), (224, KR, 32)])
WIN_SLOTS.append([(0, CKV, 64), (64, KR + 16, 16), (80, KR, 16), (96, KR + 16, 16), (112, KR, 16), (128, CKV, 128)])
WIN_SLOTS.append([(0, AV, 256)])
WIN_SLOTS.append([(0, AV + 256, 256)])
WIN_SLOTS.append([(0, BV, 128), (128, BV, 128)])
NWS = len(WIN_SLOTS)


def build(NSEQ, S, depth=DEPTH, stop_after=None):
    NT = S // 512
    NB = 1 + S // 128
    Lp = 128 + S
    KCH = 13 if NB % 13 == 0 else (5 if NB % 5 == 0 else 1)
    NCH = NB // KCH
    nc = bass.Bass("TRN2", target_bir_lowering=False)
    es = contextlib.ExitStack()
    P = Prog(nc, es)
    pe, act, dve, pool, sp = P.pe, P.act, P.dve, P.pool, P.sp

    def din(name, shape, dt=F32):
        return nc.dram_tensor(name, list(shape), dt, kind="ExternalInput").ap()

    def dscr(name, shape, dt):
        return nc.dram_tensor(name, list(shape), dt, kind="Internal").ap()

    x_in = din("x", [NSEQ, S, D])
    meta_in = din("meta", [NMETA, D])
    rb_in = din("rel_bias", [32, 8])
    g_ffn1 = din("g_ffn1", [depth, D]); w_f1gu = din("w_ffn1_gu", [depth, D, 2 * DFF]); w_f1d = din("w_ffn1_down", [depth, DFF, D])
    g_mix = din("g_mix", [depth, D]); w_in = din("w_in", [depth, D, 2464])
    dlam = din("diff_lambda", [depth, 4, 64]); g_subln = din("g_subln", [depth, 128]); sinks = din("sinks", [depth, 4])
    g_cq = din("g_cq", [depth, 256]); g_ckv = din("g_ckv", [depth, 128])
    w_uq = din("w_uq", [depth, 256, 384]); w_ukv = din("w_ukv", [depth, 128, 512]); w_out = din("w_out", [depth, D, D])
    g_ffn2 = din("g_ffn2", [depth, D]); w_f2gu = din("w_ffn2_gu", [depth, D, 2 * DFF]); w_f2d = din("w_ffn2_down", [depth, DFF, D])
    g_final = din("g_final", [D])
    c_bk = din("c_bk", [128, 384]); c_mb = din("c_mb", [128, 384]); c_negb = din("c_negb", [128, 384])
    c_cos = din("c_cos", [32, Lp]); c_ss = din("c_ss", [32, Lp]); c_ident = din("c_ident", [128, 128])
    y_out = nc.dram_tensor("y", [NSEQ, S, D], F32, kind="ExternalOutput").ap()

    GU = dscr("s_gu", [2 * depth, NFC, 128, 2048], BF16)
    DN = dscr("s_dn", [2 * depth, KD, 128, DFF], BF16)
    WIN = dscr("s_win", [depth, NWS, 128, 2048], BF16)
    WOUT = dscr("s_wout", [depth, KD, 128, 1536], BF16)
    QA = dscr("s_qa", [NSEQ, 2, 4, 128, Lp], BF16); KA = dscr("s_ka", [NSEQ, 2, 4, 128, Lp], BF16)
    VA = dscr("s_va", [NSEQ, 2, 4, 128, NB, 128], BF16)
    QB = dscr("s_qb", [NSEQ, 2, 2, 128, Lp], BF16); KB = dscr("s_kb", [NSEQ, 2, 2, 128, Lp], BF16)
    VB = dscr("s_vb", [NSEQ, 2, 128, NB, 130], BF16)
    QC = dscr("s_qc", [NSEQ, 2, 4, 128, Lp], BF16); KC = dscr("s_kc", [NSEQ, 2, 4, 128, Lp], BF16)
    VC = dscr("s_vc", [NSEQ, 2, 4, 128, NB, 65], BF16)
    HS = dscr("s_h", [NSEQ, KD, 128, Lp], F32)
    dbufs = {}

    def db(*key):
        if key not in dbufs:
            dbufs[key] = Buf()
        return dbufs[key]

    def tile_cols(t):
        return (0, 128) if t == 0 else (128 + (t - 1) * 512, 512)

    def tile_blocks(t):
        return (0, 1) if t == 0 else (1 + 4 * (t - 1), 4)

    ntiles = NT + 1

    ident = P.sb("ident", [128, 128], F32)
    ones_f = P.sb("ones_f", [128, 128], F32)
    sel65 = P.sb("sel65", [128, 64], F32)
    epsc = P.sb("epsc", [128, 1], F32)
    gcol = P.sb("gcol", [128, depth * 3 * KD + KD + depth * 4], F32)
    rbb = P.sb("rbb", [128, 256], F32)
    TA = P.sb("TA", [128, 4, 384], F32); TB = P.sb("TB", [128, 4, 384], F32)
    nlam = P.sb("nlam", [128, depth], F32); esink = P.sb("esink", [128, depth * 4], F32)
    wuq = P.sb("wuq", [128, depth, 2, 384], BF16); wuqs = P.sb("wuqs", [128, depth, 2, 384], BF16)
    wukv = P.sb("wukv", [128, depth, 512], BF16)

    def gc(kind, l):
        o = (l * 3 + kind) * KD
        return gcol[:, o:o + KD]
    GFIN = depth * 3 * KD
    GSM = GFIN + KD

    h = P.sb("h", [128, KD, 512], F32)
    xn = P.sb("xn", [128, KD, 512], BF16)
    aT = P.sb("aT", [128, NFC, 512], BF16)
    big = P.sb("big", [128, 4096], F32)
    sq = [P.sb(f"sq{i}", [128, 512], F32) for i in range(2)]
    rs = P.sb("rs", [128, 512], F32)
    sg = [P.sb(f"sg{i}", [128, 512], F32) for i in range(2)]
    cbk, cmb, cnegb = Al(lambda: sq[0][:, 0:384]), Al(lambda: sq[1][:, 0:384]), Al(lambda: sg[0][:, 0:384])
    ringG = Ring(P, "rg", [128, 2048], BF16, 3)
    ringD = Ring(P, "rd", [128, DFF], BF16, 3)
    ringK = Ring(P, "rk", [128, KCH * 128], BF16, 2)
    ringV = Ring(P, "rv", [128, KCH * 128], BF16, 2)
    qa = P.sb("qa", [128, 4, 512], BF16); qb = P.sb("qb", [128, 2, 512], BF16); qc = P.sb("qc", [128, 4, 512], BF16)
    kbn = P.sb("kbn", [128, 2, 768], BF16); vbn = P.sb("vbn", [128, 6, 130], BF16)
    cat = P.sb("cat", [128, 12, 512], BF16)
    acc2 = P.sb("acc2", [128, 2, 512], F32)
    ones_b = P.sb("ones_b", [128, 128], BF16)
    NE = 3
    Et = [P.sb(f"E{i}", [128, 2, 512], BF16) for i in range(NE)]
    Eh = [[Buf(), Buf()] for _ in range(NE)]
    tmpb = [P.sb(f"tmpb{i}", [128, 512], F32) for i in range(2)]
    t0 = P.sb("t0", [128, 512], F32); t1 = P.sb("t1", [128, 512], F32)
    rr = [P.sb(f"rr{i}", [128, 512], F32) for i in range(2)]
    osb = P.sb("osb", [128, 512], F32)
    stg = [Al(lambda: aT[:, 0:4, :]), Al(lambda: aT[:, 4:8, :])]
    vstg = Al(lambda: aT[:, 8:12, :])
    vbstg = P.sb("vbstg", [128, 4, 130], BF16)
    vcstg = P.sb("vcstg", [128, 4, 4, 65], BF16)
    cqn = P.sb("cqn", [128, 2, 512], BF16); ckvn = P.sb("ckvn", [128, 512], BF16)
    ropec = P.sb("ropec", [128, 2, 512], F32)
    krope = P.sb("krope", [128, 512], BF16)
    psall = P.ps("psall", [128, 8, 512])
    pb = [Al(lambda i=i: psall[:, i, :]) for i in range(8)]
    xnb = [Buf() for _ in range(KD)]
    aTb = [Buf() for _ in range(NFC)]
    hb_ = [Buf() for _ in range(KD)]

    def mm(out, lhsT, rhs, start, stop, r, w):
        P.op(pe, lambda: nc.tensor.matmul(out, lhsT, rhs, start=start, stop=stop), r=r, w=w)

    def recip(out, in_, scratch, r, w, bias=None):
        w1 = [x for x in w if x is not None][:1]
        P.op(act, lambda: nc.scalar.activation(out=out, in_=in_, func=AF.Ln, **({"bias": bias} if bias is not None else {})), r=r, w=w1)
        P.op(act, lambda: nc.scalar.activation(out=out, in_=out, func=AF.Exp, scale=-1.0), r=w1, w=w1)

    def rms_scale(srcs, D_, W, r):
        n = len(srcs)
        for i, sap in enumerate(srcs):
            s_ = sq[i % 2]
            P.op(act, lambda: nc.scalar.activation(out=s_[:, :W], in_=sap, func=AF.Square), r=(r[i] if isinstance(r[0], list) else r), w=[s_])
            mm(pb[7][:, :W], ones_f[:, :], s_[:, :W], i == 0, i == n - 1, [s_, ones_f], [pb[7]])
        P.op(act, lambda: nc.scalar.activation(out=rs[:, :W], in_=pb[7][:, :W], func=AF.Ln, scale=1.0 / D_, bias=epsc[:, 0:1]), r=[pb[7], epsc], w=[rs])
        P.op(act, lambda: nc.scalar.activation(out=rs[:, :W], in_=rs[:, :W], func=AF.Exp, scale=-0.5), r=[rs], w=[rs])

    def rmsnorm_h(gap, W):
        rms_scale([h[:, k, :W] for k in range(KD)], D, W, [[hb_[k]] for k in range(KD)])
        for k in range(KD):
            P.op(dve, lambda: nc.vector.scalar_tensor_tensor(out=xn[:, k, :W], in0=h[:, k, :W], scalar=gap[:, k:k + 1],
                                                             in1=rs[:, :W], op0=ALU.mult, op1=ALU.mult), r=[hb_[k], rs, gcol], w=[xnb[k]])

    def ffn(f, gap, W):
        rmsnorm_h(gap, W)
        for c in range(NFC):
            sl = ringG.pop()
            par = c % 2
            pg, pu = pb[2 * par], pb[2 * par + 1]
            for part, pt in ((0, pg), (1, pu)):
                for k in range(KD):
                    mm(pt[:, :W], sl[:, k * 256 + part * 128:k * 256 + part * 128 + 128], xn[:, k, :W], k == 0, k == KD - 1, [sl, xnb[k]], [pt])
            s_ = sg[par]
            P.op(act, lambda: nc.scalar.activation(out=s_[:, :W], in_=pg[:, :W], func=AF.Silu), r=[pg], w=[s_])
            P.op(dve, lambda: nc.vector.tensor_tensor(out=aT[:, c, :W], in0=s_[:, :W], in1=pu[:, :W], op=ALU.mult), r=[s_, pu], w=[aTb[c], aT])
        for d in range(KD):
            sl = ringD.pop()
            po = pb[4 + d % 2]
            for c in range(NFC):
                mm(po[:, :W], sl[:, c * 128:(c + 1) * 128], aT[:, c, :W], c == 0, c == NFC - 1, [sl, aTb[c]], [po])
            P.op(dve, lambda: nc.vector.scalar_tensor_tensor(out=h[:, d, :W], in0=po[:, :W], scalar=0.5, in1=h[:, d, :W],
                                                             op0=ALU.mult, op1=ALU.add), r=[po, hb_[d]], w=[hb_[d]])

    def plan_ffn(f):
        for c in range(NFC):
            ringG.plan.append(lambda sl, c=c: P.dma(sp, sl[:, :], GU[f, c], r=[db("GU", f)], w=[sl]))
        for d in range(KD):
            ringD.plan.append(lambda sl, d=d: P.dma(sp, sl[:, :], DN[f, d], r=[db("DN", f)], w=[sl]))

    def plan_win(l):
        for s_ in range(NWS):
            ringG.plan.append(lambda sl, s_=s_: P.dma(sp, sl[:, :], WIN[l, s_], r=[db("WIN", l)], w=[sl]))

    def plan_wout(l):
        for d in range(KD):
            ringD.plan.append(lambda sl, d=d: P.dma(sp, sl[:, 0:1536], WOUT[l, d], r=[db("WOUT", l)], w=[sl]))

    evac_i = [0]

    def evac(out, in_, r, w):
        evac_i[0] += 1
        if evac_i[0] % 2 == 0:
            P.op(act, lambda: nc.scalar.activation(out=out, in_=in_, func=AF.Copy), r=r, w=w)
        else:
            P.op(dve, lambda: nc.vector.tensor_copy(out=out, in_=in_), r=r, w=w)

    def prologue():
        P.dma(sp, ident[:, :], c_ident, w=[ident])
        P.dma(sp, cbk[:, :], c_bk, w=[cbk, sq[0]]); P.dma(sp, cmb[:, :], c_mb, w=[cmb, sq[1]]); P.dma(sp, cnegb[:, :], c_negb, w=[cnegb, sg[0]])
        P.op(dve, lambda: nc.vector.memset(ones_f[:, :], 1.0), w=[ones_f])
        P.op(dve, lambda: nc.vector.memset(epsc[:, :], EPS), w=[epsc])
        P.op(dve, lambda: nc.vector.memset(ones_b[:, :], 1.0), w=[ones_b])
        P.op(dve, lambda: nc.vector.memset(sel65[:, :], 0.0), w=[sel65])
        P.op(dve, lambda: nc.vector.memset(sel65[64:65, :], 1.0), w=[sel65])
        P.dma(sp, rbb[:, :], rb_in.rearrange("b h -> (b h)").partition_broadcast(128), w=[rbb])
        def gload(dst, src1d, k):
            P.dma(sp, dst, src1d.rearrange("(k p) -> p k", p=128), w=[gcol], allow_slow_non_contiguous=True)
        for l in range(depth):
            gload(gc(0, l), g_ffn1[l], KD); gload(gc(1, l), g_mix[l], KD); gload(gc(2, l), g_ffn2[l], KD)
            o = GSM + 4 * l
            gload(gcol[:, o:o + 2], g_cq[l], 2); gload(gcol[:, o + 2:o + 3], g_ckv[l], 1); gload(gcol[:, o + 3:o + 4], g_subln[l], 1)
        gload(gcol[:, GFIN:GFIN + KD], g_final, KD)
        dl = big
        for l in range(depth):
            P.dma(sp, big[:, 0:256], dlam[l].rearrange("a b -> (a b)").partition_broadcast(128), w=[big])
            P.op(dve, lambda: nc.vector.tensor_tensor(out=big[:, 256:320], in0=big[:, 0:64], in1=big[:, 64:128], op=ALU.mult), r=[big], w=[big])
            P.op(dve, lambda: nc.vector.tensor_tensor(out=big[:, 320:384], in0=big[:, 128:192], in1=big[:, 192:256], op=ALU.mult), r=[big], w=[big])
            P.op(dve, lambda: nc.vector.tensor_reduce(out=big[:, 384:385], in_=big[:, 256:320], axis=AX.X, op=ALU.add), r=[big], w=[big])
            P.op(dve, lambda: nc.vector.tensor_reduce(out=big[:, 385:386], in_=big[:, 320:384], axis=AX.X, op=ALU.add), r=[big], w=[big])
            P.op(act, lambda: nc.scalar.activation(out=big[:, 386:388], in_=big[:, 384:386], func=AF.Exp), r=[big], w=[big])
            lam_init = 0.8 - 0.6 * math.exp(-0.3 * l)
            P.op(dve, lambda: nc.vector.tensor_tensor(out=big[:, 388:389], in0=big[:, 387:388], in1=big[:, 386:387], op=ALU.subtract), r=[big], w=[big])
            P.op(dve, lambda: nc.vector.tensor_scalar(out=nlam[:, l:l + 1], in0=big[:, 388:389], scalar1=-lam_init, scalar2=None, op0=ALU.add), r=[big], w=[nlam])
            P.dma(sp, big[:, 400:404], sinks[l].partition_broadcast(128), w=[big])
            P.op(act, lambda: nc.scalar.activation(out=esink[:, 4 * l:4 * l + 4], in_=big[:, 400:404], func=AF.Exp), r=[big], w=[esink])
        m_ = tmpb[0]
        for hh in range(8):
            dst = TA[:, hh, :] if hh < 4 else TB[:, hh - 4, :]
            dstb = TA if hh < 4 else TB
            for b in range(32):
                o_ = dst if b == 0 else m_[:, 0:384]
                P.op(dve, lambda: nc.vector.tensor_scalar(out=o_, in0=cbk[:, :], scalar1=float(b), scalar2=rbb[:, b * 8 + hh:b * 8 + hh + 1],
                                                          op0=ALU.is_equal, op1=ALU.mult), r=[cbk, rbb], w=[dstb if b == 0 else m_])
                if b > 0:
                    P.op(dve, lambda: nc.vector.tensor_tensor(out=dst, in0=dst, in1=m_[:, 0:384], op=ALU.add), r=[m_, dstb], w=[dstb])
            if hh >= 4:
                P.op(dve, lambda: nc.vector.tensor_tensor(out=dst, in0=dst, in1=cmb[:, :], op=ALU.mult), r=[cmb, dstb], w=[dstb])
                P.op(dve, lambda: nc.vector.tensor_tensor(out=dst, in0=dst, in1=cnegb[:, :], op=ALU.add), r=[cnegb, dstb], w=[dstb])
        for l in range(depth):
            P.dma(pool, wuq[:, l, :, :], w_uq[l].rearrange("(kc p) n -> p kc n", p=128), w=[wuq])
            P.dma(pool, wukv[:, l, :], w_ukv[l], w=[wukv])
            for hh in range(4):
                src = w_uq[l].rearrange("(kc p) n -> p kc n", p=128)
                o = hh * 96
                P.dma(pool, wuqs[:, l, :, o:o + 64], src[:, :, o:o + 64], w=[wuqs])
                P.dma(pool, wuqs[:, l, :, o + 64:o + 80], src[:, :, o + 80:o + 96], w=[wuqs])
                P.dma(pool, wuqs[:, l, :, o + 80:o + 96], src[:, :, o + 64:o + 80], w=[wuqs])
        P.barrier()
        srct = [Al(lambda: big[:, 0:2816]), Al(lambda: h[:, :, :].rearrange("p k n -> p (k n)")[:, 0:2816])]
        dstt = [Al(lambda: aT[:, :, :].rearrange("p c n -> p (c n)")[:, 0:2816]), Al(lambda: aT[:, :, :].rearrange("p c n -> p (c n)")[:, 2816:5632])]
        cnt = [0]
        cast_engs = [dve, act, pool]

        def conv(pieces, n, dst_ap, wbuf):
            i = cnt[0] % 2
            cnt[0] += 1
            for (view_fn, src_ap) in pieces:
                P.dma(sp, view_fn(srct[i]), src_ap, w=[srct[i]])
            E = cast_engs[cnt[0] % 3]
            if E is act:
                P.op(act, lambda: nc.scalar.activation(out=dstt[i][:, 0:n], in_=srct[i][:, 0:n], func=AF.Copy), r=[srct[i]], w=[dstt[i]])
            elif E is dve:
                P.op(dve, lambda: nc.vector.tensor_copy(out=dstt[i][:, 0:n], in_=srct[i][:, 0:n]), r=[srct[i]], w=[dstt[i]])
            else:
                P.op(pool, lambda: nc.gpsimd.tensor_copy(out=dstt[i][:, 0:n], in_=srct[i][:, 0:n]), r=[srct[i]], w=[dstt[i]])
            P.dma(act, dst_ap, dstt[i][:, 0:n], r=[dstt[i]], w=[wbuf])

        for l in range(depth):
            for fi, (wgu, wdn) in enumerate(((w_f1gu, w_f1d), (w_f2gu, w_f2d))):
                f = 2 * l + fi
                for c in range(NFC):
                    pcs = []
                    for part in range(2):
                        src = wgu[l][:, part * DFF + c * 128: part * DFF + (c + 1) * 128].rearrange("(k p) j -> p k j", p=128)
                        pcs.append((lambda sb_, part=part: sb_[:, 0:2048].rearrange("p (k j) -> p k j", j=256)[:, :, part * 128:(part + 1) * 128], src))
                    conv(pcs, 2048, GU[f, c], db("GU", f))
                for d in range(KD):
                    src = wdn[l][:, d * 128:(d + 1) * 128].rearrange("(c p) j -> p c j", p=128)
                    conv([(lambda sb_: sb_[:, 0:DFF].rearrange("p (c j) -> p c j", j=128), src)], DFF, DN[f, d], db("DN", f))
            for s_, pieces in enumerate(WIN_SLOTS):
                pcs = []
                for (do, so, n) in pieces:
                    src = w_in[l][:, so:so + n].rearrange("(k p) j -> p k j", p=128)
                    pcs.append((lambda sb_, do=do, n=n: sb_[:, 0:2048].rearrange("p (k j) -> p k j", j=256)[:, :, do:do + n], src))
                conv(pcs, 2048, WIN[l, s_], db("WIN", l))
            for d in range(KD):
                pcs = [(lambda sb_: sb_[:, 0:512].rearrange("p (c j) -> p c j", j=128),
                        w_out[l][0:512, d * 128:(d + 1) * 128].rearrange("(c p) j -> p c j", p=128)),
                       (lambda sb_: sb_[0:64, 512:1536].rearrange("p (c j) -> p c j", j=128),
                        w_out[l][512:1024, d * 128:(d + 1) * 128].rearrange("(c p) j -> p c j", p=64))]
                conv(pcs, 1536, WOUT[l, d], db("WOUT", l))
        P.op(dve, lambda: nc.vector.memset(vbstg[:, :, :], 1.0), w=[vbstg])
        P.op(dve, lambda: nc.vector.memset(vcstg[:, :, :, :], 1.0), w=[vcstg])
        P.op(dve, lambda: nc.vector.memset(cat[:, :, :], 0.0), w=[cat])
        P.op(dve, lambda: nc.vector.memset(qc[:, :, :], 0.0), w=[qc])

    def load_x(s, t):
        c0, W = tile_cols(t)
        xin = big
        if t == 0:
            P.op(dve, lambda: nc.vector.memset(big[:, 0:1024], 0.0), w=[big])
            P.dma(sp, big[MS:128, 0:1024], meta_in, w=[big])
            nb = 1
        else:
            tok0 = (t - 1) * 512
            P.dma(sp, big[:, :].rearrange("p (b n) -> p b n", n=1024), x_in[s, tok0:tok0 + 512, :].rearrange("(b p) n -> p b n", p=128), w=[big])
            nb = 4
        for k in range(KD):
            pt = pb[k % 2]
            for b in range(nb):
                P.op(pe, lambda: nc.tensor.transpose(pt[:, b * 128:(b + 1) * 128], big[:, b * 1024 + k * 128: b * 1024 + (k + 1) * 128], ident[:, :]),
                     r=[big, ident], w=[pt])
            evac(h[:, k, :W], pt[:, :W], [pt], [hb_[k]])

    def load_h(s, t):
        c0, W = tile_cols(t)
        P.dma(sp, h[:, :, :W], HS[s, :, :, c0:c0 + W].rearrange("k p n -> p k n"), r=[db("H", s, t)], w=hb_)

    def store_h(s, t):
        c0, W = tile_cols(t)
        P.dma(pool, HS[s, :, :, c0:c0 + W].rearrange("k p n -> p k n"), h[:, :, :W], r=hb_, w=[db("H", s, t)])

    def pre(s, t, l):
        c0, W = tile_cols(t)
        b0, nb = tile_blocks(t)
        bq_ = l % 2
        ffn(2 * l, gc(0, l), W)
        rmsnorm_h(gc(1, l), W)
        P.dma(sp, ropec[64:96, 0, :W], c_cos[:, c0:c0 + W], w=[ropec])
        P.dma(sp, ropec[64:96, 1, :W], c_ss[:, c0:c0 + W], w=[ropec])
        pi = [0]
        alias_begin(aT, stg + [vstg])

        def proj(sl, half):
            pt = pb[pi[0] % 2]
            pi[0] += 1
            for k in range(KD):
                mm(pt[:, :W], sl[:, k * 256 + half * 128:k * 256 + half * 128 + 128], xn[:, k, :W], k == 0, k == KD - 1, [sl, xnb[k]], [pt])
            return pt
        for kind, dst in ((0, QA), (1, KA)):
            st = stg[kind]
            for i in range(2):
                sl = ringG.pop()
                for half in range(2):
                    pt = proj(sl, half)
                    evac(st[:, 2 * i + half, :W], pt[:, :W], [pt], [st])
            P.dma(pool, dst[s, bq_, :, :, c0:c0 + W].rearrange("h p n -> p h n"), st[:, :, :W], r=[st], w=[db("QA" if kind == 0 else "KA", s, bq_, t)])
        for kind, dst in ((0, QB), (1, KB)):
            st = stg[kind]
            sl = ringG.pop()
            for half in range(2):
                pt = proj(sl, half)
                evac(st[:, half, :W], pt[:, :W], [pt], [st])
            P.dma(pool, dst[s, bq_, :, :, c0:c0 + W].rearrange("h p n -> p h n"), st[:, 0:2, :W], r=[st], w=[db("QB" if kind == 0 else "KB", s, bq_, t)])
        sl = ringG.pop()
        for half in range(2):
            for k in range(KD):
                mm(pb[2 + half][:, :W], sl[:, k * 256 + half * 128:k * 256 + half * 128 + 128], xn[:, k, :W], k == 0, k == KD - 1, [sl, xnb[k]], [pb[2 + half]])
        sl = ringG.pop()
        for half in range(2):
            for k in range(KD):
                mm(pb[4 + half][:, :W], sl[:, k * 256 + half * 128:k * 256 + half * 128 + 128], xn[:, k, :W], k == 0, k == KD - 1, [sl, xnb[k]], [pb[4 + half]])
        sl = ringG.pop()
        for k in range(KD):
            mm(pb[6][:, :W], sl[:, k * 256:k * 256 + 128], xn[:, k, :W], k == 0, k == KD - 1, [sl, xnb[k]], [pb[6]])
        go = GSM + 4 * l
        rms_scale([pb[2][:, :W], pb[3][:, :W]], 256, W, [pb[2], pb[3]])
        for i in range(2):
            P.op(dve, lambda: nc.vector.scalar_tensor_tensor(out=cqn[:, i, :W], in0=pb[2 + i][:, :W], scalar=gcol[:, go + i:go + i + 1],
                                                             in1=rs[:, :W], op0=ALU.mult, op1=ALU.mult), r=[pb[2 + i], rs, gcol], w=[cqn])
        rms_scale([pb[4][:, :W]], 128, W, [pb[4]])
        P.op(dve, lambda: nc.vector.scalar_tensor_tensor(out=ckvn[:, :W], in0=pb[4][:, :W], scalar=gcol[:, go + 2:go + 3],
                                                         in1=rs[:, :W], op0=ALU.mult, op1=ALU.mult), r=[pb[4], rs, gcol], w=[ckvn])
        P.op(dve, lambda: nc.vector.tensor_tensor(out=t0[64:96, :W], in0=pb[5][64:96, :W], in1=ropec[64:96, 0, :W], op=ALU.mult), r=[pb[5], ropec], w=[t0])
        P.op(dve, lambda: nc.vector.tensor_tensor(out=t1[64:96, :W], in0=pb[6][64:96, :W], in1=ropec[64:96, 1, :W], op=ALU.mult), r=[pb[6], ropec], w=[t1])
        P.op(dve, lambda: nc.vector.tensor_tensor(out=krope[64:96, :W], in0=t0[64:96, :W], in1=t1[64:96, :W], op=ALU.add), r=[t0, t1], w=[krope])
        st = stg[1]
        for hh in range(4):
            pt = pb[hh % 2]
            mm(pt[0:64, :W], wukv[:, l, hh * 128:hh * 128 + 64], ckvn[:, :W], True, True, [wukv, ckvn], [pt])
            evac(st[0:64, hh, :W], pt[0:64, :W], [pt], [st])
            P.op(pool, lambda: nc.gpsimd.tensor_copy(out=st[64:96, hh, :W], in_=krope[64:96, :W]), r=[krope], w=[st])
        P.dma(pool, KC[s, bq_, :, 0:96, c0:c0 + W].rearrange("h p n -> p h n"), st[0:96, :, :W], r=[st], w=[db("KC", s, bq_, t)])
        st = stg[0]
        for hh in range(4):
            pa_, pb_ = pb[2], pb[3]
            for kc in range(2):
                mm(pa_[0:96, :W], wuq[:, l, kc, hh * 96:(hh + 1) * 96], cqn[:, kc, :W], kc == 0, kc == 1, [wuq, cqn], [pa_])
            for kc in range(2):
                mm(pb_[0:96, :W], wuqs[:, l, kc, hh * 96:(hh + 1) * 96], cqn[:, kc, :W], kc == 0, kc == 1, [wuqs, cqn], [pb_])
            evac(st[0:64, hh, :W], pa_[0:64, :W], [pa_], [st])
            P.op(dve, lambda: nc.vector.tensor_tensor(out=t0[64:96, :W], in0=pa_[64:96, :W], in1=ropec[64:96, 0, :W], op=ALU.mult), r=[pa_, ropec], w=[t0])
            P.op(dve, lambda: nc.vector.tensor_tensor(out=t1[64:96, :W], in0=pb_[64:96, :W], in1=ropec[64:96, 1, :W], op=ALU.mult), r=[pb_, ropec], w=[t1])
            P.op(dve, lambda: nc.vector.tensor_tensor(out=st[64:96, hh, :W], in0=t0[64:96, :W], in1=t1[64:96, :W], op=ALU.add), r=[t0, t1], w=[st])
        P.dma(pool, QC[s, bq_, :, 0:96, c0:c0 + W].rearrange("h p n -> p h n"), st[0:96, :, :W], r=[st], w=[db("QC", s, bq_, t)])
        for b in range(nb):
            pt = pb[b % 2]
            mm(pt[:, 0:256], ckvn[:, b * 128:(b + 1) * 128], wukv[:, l, :].rearrange("p (h j) -> p h j", j=128)[:, :, 64:128], True, True, [wukv, ckvn], [pt])
            evac(vcstg[:, b, :, 0:64], pt[:, 0:256].rearrange("p (h j) -> p h j", j=64), [pt], [vcstg])
        for hh in range(4):
            P.dma(pool, VC[s, bq_, hh, :, b0:b0 + nb, :], vcstg[:, 0:nb, hh, :], r=[vcstg], w=[db("VC", s, bq_, t)])
        for half in range(2):
            sl = ringG.pop()
            for b in range(nb):
                pt = pb[b % 2]
                for k in range(KD):
                    mm(pt[:, 0:256], xn[:, k, b * 128:(b + 1) * 128], sl[:, k * 256:(k + 1) * 256], k == 0, k == KD - 1, [sl, xnb[k]], [pt])
                evac(vstg[:, b, half * 256:(half + 1) * 256], pt[:, 0:256], [pt], [vstg])
        for hh in range(4):
            P.dma(pool, VA[s, bq_, hh, :, b0:b0 + nb, :], vstg[:, 0:nb, hh * 128:(hh + 1) * 128], r=[vstg], w=[db("VA", s, bq_, t)])
        sl = ringG.pop()
        for b in range(nb):
            pt = pb[b % 2]
            for k in range(KD):
                mm(pt[:, 0:128], xn[:, k, b * 128:(b + 1) * 128], sl[:, k * 256:k * 256 + 128], k == 0, k == KD - 1, [sl, xnb[k]], [pt])
            evac(vbstg[:, b, :].rearrange("p (g j) -> p g j", j=65)[:, :, 0:64], pt[:, 0:128].rearrange("p (g j) -> p g j", j=64), [pt], [vbstg])
        P.dma(pool, VB[s, bq_, :, b0:b0 + nb, :], vbstg[:, 0:nb, :], r=[vbstg], w=[db("VB", s, bq_, t)])
        alias_end(aT, stg + [vstg])

    def plan_attn(s, t, l):
        bq_ = l % 2
        allk = lambda kind: [db(kind, s, bq_, tt) for tt in range(ntiles)]
        for (kd, vd, kn, vn, vw) in ((KA, VA, "KA", "VA", 128), (KC, VC, "KC", "VC", 65)):
            for hh in range(4):
                for ch in range(NCH):
                    ringK.plan.append(lambda sl, kd=kd, kn=kn, hh=hh, ch=ch: P.dma(
                        sp, sl[:, :], kd[s, bq_, hh, :, ch * KCH * 128:(ch + 1) * KCH * 128], r=allk(kn), w=[sl]))
                    ringV.plan.append(lambda sl, vd=vd, vn=vn, hh=hh, ch=ch, vw=vw: P.dma(
                        sp, sl[:, 0:KCH * vw].rearrange("p (b j) -> p b j", j=vw), vd[s, bq_, hh, :, ch * KCH:(ch + 1) * KCH, :], r=allk(vn), w=[sl]))

    def attn(s, t, l):
        c0, W = tile_cols(t)
        b0, nb = tile_blocks(t)
        bq_ = l % 2
        P.dma(sp, qa[:, :, :W], QA[s, bq_, :, :, c0:c0 + W].rearrange("h p n -> p h n"), r=[db("QA", s, bq_, t)], w=[qa])
        P.dma(sp, qb[:, :, :W], QB[s, bq_, :, :, c0:c0 + W].rearrange("h p n -> p h n"), r=[db("QB", s, bq_, t)], w=[qb])
        P.dma(sp, qc[0:96, :, :W], QC[s, bq_, :, 0:96, c0:c0 + W].rearrange("h p n -> p h n"), r=[db("QC", s, bq_, t)], w=[qc])
        go = GSM + 4 * l
        lam_init = 0.8 - 0.6 * math.exp(-0.3 * l)
        ei = [0, 0]
        tbi = [0]

        def bias_exp(S_, Eb, Eap, j, kind, hh, scale):
            if kind == "C":
                if j == 0:
                    tb = tmpb[tbi[0] % 2]
                    tbi[0] += 1
                    P.op(dve, lambda: nc.vector.tensor_scalar(out=tb[:, :W], in0=S_[:, :W], scalar1=scale, scalar2=None, op0=ALU.mult),
                         r=[S_], w=[tb])
                    P.op(dve, lambda: nc.vector.memset(tb[0:MS, :W], NEG), w=[tb])
                    P.op(act, lambda: nc.scalar.activation(out=Eap, in_=tb[:, :W], func=AF.Exp), r=[tb], w=[Eb])
                else:
                    P.op(act, lambda: nc.scalar.activation(out=Eap, in_=S_[:, :W], func=AF.Exp, scale=scale), r=[S_], w=[Eb])
                return
            near = (j == 0) or (b0 - 1 <= j <= b0 + nb)
            if not near:
                col = rbb[:, 15 * 8 + hh:15 * 8 + hh + 1] if j < b0 else rbb[:, 31 * 8 + hh:31 * 8 + hh + 1]
                P.op(act, lambda: nc.scalar.activation(out=Eap, in_=S_[:, :W], func=AF.Exp, bias=col, scale=scale), r=[S_, rbb], w=[Eb])
                return
            tb = tmpb[tbi[0] % 2]
            tbi[0] += 1
            for qi in range(nb):
                dl_ = j - (b0 + qi)
                cs = slice(qi * 128, (qi + 1) * 128)
                if -1 <= dl_ <= 1:
                    tab = TA[:, hh, (1 - dl_) * 128:(2 - dl_) * 128]
                    P.op(dve, lambda: nc.vector.scalar_tensor_tensor(out=tb[:, cs], in0=S_[:, cs], scalar=scale, in1=tab, op0=ALU.mult, op1=ALU.add),
                         r=[S_, TA], w=[tb])
                else:
                    col = rbb[:, 15 * 8 + hh:15 * 8 + hh + 1] if dl_ < 0 else rbb[:, 31 * 8 + hh:31 * 8 + hh + 1]
                    P.op(dve, lambda: nc.vector.tensor_scalar(out=tb[:, cs], in0=S_[:, cs], scalar1=scale, scalar2=col, op0=ALU.mult, op1=ALU.add),
                         r=[S_, rbb], w=[tb])
            if j == 0:
                P.op(dve, lambda: nc.vector.memset(tb[0:MS, :W], NEG), w=[tb])
            P.op(act, lambda: nc.scalar.activation(out=Eap, in_=tb[:, :W], func=AF.Exp), r=[tb], w=[Eb])

        scaleA = 64 ** -0.5
        units = [(ch, jj) for ch in range(NCH) for jj in range(KCH)]
        n = len(units)
        pend = {"p1": None, "p2": None}

        def run_p1():
            if pend["p1"] is not None:
                f = pend["p1"]
                pend["p1"] = None
                f()

        def run_p2():
            run_p1()
            if pend["p2"] is not None:
                f = pend["p2"]
                pend["p2"] = None
                f()

        def a_fin1(hh, npool):
            O = [pb[4], pb[5]]
            P.op(act, lambda: nc.scalar.activation(out=t0[:, :W], in_=O[0][:, :W], func=AF.Copy), r=[O[0]], w=[t0])
            P.op(dve, lambda: nc.vector.tensor_copy(out=t1[:, :W], in_=O[1][:, :W]), r=[O[1]], w=[t1])
            for m in range(2):
                mm(pb[6 + m][:, :W], ones_f[:, :], acc2[:, m, :W], npool == 0, True, [ones_f, acc2], [pb[6 + m]])
                recip(rr[m][:, :W], pb[6 + m][:, :W], None, [pb[6 + m]], [rr[m]])
            P.op(dve, lambda: nc.vector.tensor_tensor(out=t0[:, :W], in0=t0[:, :W], in1=rr[0][:, :W], op=ALU.mult), r=[t0, rr[0]], w=[t0])
            P.op(dve, lambda: nc.vector.tensor_tensor(out=t1[:, :W], in0=t1[:, :W], in1=rr[1][:, :W], op=ALU.mult), r=[t1, rr[1]], w=[t1])
            P.op(dve, lambda: nc.vector.scalar_tensor_tensor(out=osb[:, :W], in0=t1[:, :W], scalar=nlam[:, l:l + 1], in1=t0[:, :W], op0=ALU.mult, op1=ALU.add),
                 r=[t0, t1, nlam], w=[osb])

        def a_fin2(hh):
            rms_scale([osb[:, :W]], 128, W, [osb])
            P.op(dve, lambda: nc.vector.tensor_scalar(out=rs[:, :W], in0=rs[:, :W], scalar1=(1.0 - lam_init), scalar2=None, op0=ALU.mult), r=[rs], w=[rs])
            P.op(dve, lambda: nc.vector.scalar_tensor_tensor(out=cat[:, hh, :W], in0=osb[:, :W], scalar=gcol[:, go + 3:go + 4], in1=rs[:, :W],
                                                             op0=ALU.mult, op1=ALU.mult), r=[osb, rs, gcol], w=[cat])

        def c_fin1(hh):
            evac(osb[0:65, :W], pb[4][0:65, :W], [pb[4]], [osb])

        def c_fin2(hh):
            mm(pb[6][0:64, :W], sel65[0:65, :], osb[0:65, :W], True, True, [sel65, osb], [pb[6]])
            recip(rr[0][0:64, :W], pb[6][0:64, :W], None, [pb[6]], [rr[0]])
            P.op(dve, lambda: nc.vector.tensor_tensor(out=cat[0:64, 8 + hh, :W], in0=osb[0:64, :W], in1=rr[0][0:64, :W], op=ALU.mult), r=[osb, rr[0]], w=[cat])

        P2A = min(6, n - 1)
        for hh in range(4):
            O = [pb[4], pb[5]]
            slots = {}
            vslots = {}

            def getslot(ch):
                if ch not in slots:
                    slots[ch] = ringK.pop()
                return slots[ch]

            def getv(ch):
                if ch not in vslots:
                    vslots[ch] = ringV.pop()
                return vslots[ch]

            def qk(i):
                ch, jj = units[i]
                ks = getslot(ch)
                par = i % 2
                for m in range(2):
                    S_ = pb[2 * par + m]
                    mm(S_[:, :W], ks[m * 64:(m + 1) * 64, jj * 128:(jj + 1) * 128], qa[m * 64:(m + 1) * 64, hh, :W], True, True, [ks, qa], [S_])
            qk(0)
            run_p1()
            npool = 0
            ndve = 0
            for i in range(n):
                if i == P2A:
                    run_p2()
                if i + 1 < n:
                    qk(i + 1)
                ch, jj = units[i]
                j = ch * KCH + jj
                vs = getv(ch)
                par = i % 2
                Ep = Et[ei[0] % NE]
                Eb2 = Eh[ei[0] % NE]
                ei[0] += 1
                farA = not ((j == 0) or (b0 - 1 <= j <= b0 + nb))
                if farA:
                    col = rbb[:, 15 * 8 + hh:15 * 8 + hh + 1] if j < b0 else rbb[:, 31 * 8 + hh:31 * 8 + hh + 1]
                    P.op(act, lambda: nc.scalar.activation(out=Ep[:, :, :W], in_=psall[:, 2 * par:2 * par + 2, :W], func=AF.Exp, bias=col, scale=scaleA),
                         r=[pb[2 * par], pb[2 * par + 1], rbb], w=[Eb2[0], Eb2[1]])
                for m in range(2):
                    S_ = pb[2 * par + m]
                    if not farA:
                        bias_exp(S_, Eb2[m], Ep[:, m, :W], j, "A", hh, scaleA)
                    mm(O[m][:, :W], vs[:, jj * 128:(jj + 1) * 128], Ep[:, m, :W], i == 0, i == n - 1, [vs, Eb2[m]], [O[m]])
                if i % 3 == 2 and i >= 8:
                    for m in range(2):
                        mm(pb[6 + m][:, :W], ones_b[:, :], Ep[:, m, :W], npool == 0, False, [ones_b, Eb2[m]], [pb[6 + m]])
                    npool += 1
                else:
                    if ndve == 0:
                        P.op(dve, lambda: nc.vector.tensor_copy(out=acc2[:, :, :W], in_=Ep[:, :, :W]), r=Eb2, w=[acc2])
                    else:
                        P.op(dve, lambda: nc.vector.tensor_tensor(out=acc2[:, :, :W], in0=acc2[:, :, :W], in1=Ep[:, :, :W], op=ALU.add), r=Eb2 + [acc2], w=[acc2])
                    ndve += 1
            run_p2()
            pend["p1"] = (lambda hh=hh, npool=npool: a_fin1(hh, npool))
            pend["p2"] = (lambda hh=hh: a_fin2(hh))
        scaleC = 96 ** -0.5
        LA = 2
        P2C = min(10, n - 1)
        for hh in range(4):
            O = pb[4]
            slots = {}
            vslots = {}

            def getslot(ch):
                if ch not in slots:
                    slots[ch] = ringK.pop()
                return slots[ch]

            def getv(ch):
                if ch not in vslots:
                    vslots[ch] = ringV.pop()
                return vslots[ch]

            def qk(i):
                ch, jj = units[i]
                ks = getslot(ch)
                S_ = pb[i % 4]
                mm(S_[:, :W], ks[0:96, jj * 128:(jj + 1) * 128], qc[0:96, hh, :W], True, True, [ks, qc], [S_])
            for i in range(min(LA, n)):
                qk(i)
            run_p1()
            merged = False
            for i in range(n):
                if i == P2C:
                    run_p2()
                if i + LA < n:
                    qk(i + LA)
                ch, jj = units[i]
                j = ch * KCH + jj
                vs = getv(ch)
                S_ = pb[i % 4]
                e_ = (ei[0] + i // 2) % NE
                Ep = Et[e_]
                half = i % 2
                Ebh = Eh[e_][half]
                if half == 0 and j != 0 and i + 1 < n:
                    bnk = i % 4
                    P.op(act, lambda: nc.scalar.activation(out=Ep[:, :, :W], in_=psall[:, bnk:bnk + 2, :W], func=AF.Exp, scale=scaleC),
                         r=[pb[bnk], pb[bnk + 1]], w=[Eh[e_][0], Eh[e_][1]])
                    merged = True
                elif half == 1 and merged:
                    merged = False
                else:
                    merged = False
                    bias_exp(S_, Ebh, Ep[:, half, :W], j, "C", hh, scaleC)
                mm(O[0:65, :W], vs[:, jj * 65:(jj + 1) * 65], Ep[:, half, :W], i == 0, i == n - 1, [vs, Ebh], [O])
            ei[0] += (n + 1) // 2
            run_p2()
            pend["p1"] = (lambda hh=hh: c_fin1(hh))
            pend["p2"] = (lambda hh=hh: c_fin2(hh))
        run_p2()
        jlo = max(0, b0 - 1)
        jhi = min(NB - 1, b0 + nb)
        nj = jhi - jlo + 1
        allk = lambda kind: [db(kind, s, bq_, tt) for tt in range(ntiles)]
        P.dma(sp, kbn[:, :, 0:nj * 128], KB[s, bq_, :, :, jlo * 128:(jhi + 1) * 128].rearrange("h p n -> p h n"), r=allk("KB"), w=[kbn])
        P.dma(sp, vbn[:, 0:nj, :], VB[s, bq_, :, jlo:jhi + 1, :], r=allk("VB"), w=[vbn])
        scaleB = 64 ** -0.5
        bun = []
        for hb in range(4):
            for qi in range(nb):
                qblk = b0 + qi
                js = [j for j in (qblk - 1, qblk, qblk + 1) if 0 <= j < NB]
                for ji, j in enumerate(js):
                    bun.append((hb, qi, j, ji == 0, ji == len(js) - 1))
        nbu = len(bun)

        def bqk(u):
            hb, qi, j, first, last = bun[u]
            kvh, g = hb // 2, hb % 2
            cs = slice(qi * 128, (qi + 1) * 128)
            jl = j - jlo
            S_ = pb[u % 4]
            mm(S_[:, 0:128], kbn[g * 64:(g + 1) * 64, kvh, jl * 128:(jl + 1) * 128], qb[g * 64:(g + 1) * 64, kvh, cs], True, True, [kbn, qb], [S_])
        LB = 3
        for u in range(min(LB, nbu)):
            bqk(u)
        for u in range(nbu):
            if u + LB < nbu:
                bqk(u + LB)
            hb, qi, j, first, last = bun[u]
            kvh, g = hb // 2, hb % 2
            cs = slice(qi * 128, (qi + 1) * 128)
            jl = j - jlo
            S_ = pb[u % 4]
            O = pb[4 + hb % 2]
            dl_ = j - (b0 + qi)
            tab = TB[:, hb, (1 - dl_) * 128:(2 - dl_) * 128]
            tb = tmpb[u % 2]
            P.op(dve, lambda: nc.vector.scalar_tensor_tensor(out=tb[:, 0:128], in0=S_[:, 0:128], scalar=scaleB, in1=tab, op0=ALU.mult, op1=ALU.add),
                 r=[S_, TB], w=[tb])
            if j == 0:
                P.op(dve, lambda: nc.vector.memset(tb[0:MS, 0:128], NEG), w=[tb])
            Ep = Et[ei[0] % NE]
            Ebh = Eh[ei[0] % NE][0]
            ei[0] += 1
            P.op(act, lambda: nc.scalar.activation(out=Ep[:, 0, 0:128], in_=tb[:, 0:128], func=AF.Exp), r=[tb], w=[Ebh])
            mm(O[0:65, cs], vbn[:, jl, kvh * 65:(kvh + 1) * 65], Ep[:, 0, 0:128], first, last, [vbn, Ebh], [O])
            if qi == nb - 1 and last:
                evac(osb[0:65, :W], O[0:65, :W], [O], [osb])
                mm(pb[6 + hb % 2][0:64, :W], sel65[0:65, :], osb[0:65, :W], True, True, [sel65, osb], [pb[6 + hb % 2]])
                recip(rr[0][0:64, :W], pb[6 + hb % 2][0:64, :W], None, [pb[6 + hb % 2], esink], [rr[0]], bias=esink[0:64, 4 * l + hb:4 * l + hb + 1])
                P.op(dve, lambda: nc.vector.tensor_tensor(out=cat[0:64, 4 + hb, :W], in0=osb[0:64, :W], in1=rr[0][0:64, :W], op=ALU.mult), r=[osb, rr[0]], w=[cat])

    def post(s, t, l):
        c0, W = tile_cols(t)
        for d in range(KD):
            sl = ringD.pop()
            po = pb[4 + d % 2]
            for kc in range(12):
                if kc < 4:
                    mm(po[:, :W], sl[:, kc * 128:(kc + 1) * 128], cat[:, kc, :W], kc == 0, False, [sl, cat], [po])
                else:
                    mm(po[:, :W], sl[0:64, kc * 128:(kc + 1) * 128], cat[0:64, kc, :W], False, kc == 11, [sl, cat], [po])
            P.op(dve, lambda: nc.vector.tensor_tensor(out=h[:, d, :W], in0=po[:, :W], in1=h[:, d, :W], op=ALU.add), r=[po, hb_[d]], w=[hb_[d]])
        ffn(2 * l + 1, gc(2, l), W)

    def final(s, t):
        c0, W = tile_cols(t)
        if t == 0:
            return
        rms_scale([h[:, k, :W] for k in range(KD)], D, W, [[hb_[k]] for k in range(KD)])
        for k in range(KD):
            P.op(dve, lambda: nc.vector.scalar_tensor_tensor(out=h[:, k, :W], in0=h[:, k, :W], scalar=gcol[:, GFIN + k:GFIN + k + 1],
                                                             in1=rs[:, :W], op0=ALU.mult, op1=ALU.mult), r=[hb_[k], rs, gcol], w=[hb_[k]])
        tok0 = (t - 1) * 512
        for b in range(4):
            for half in range(2):
                pt = pb[(2 * b + half) % 4]
                for kk in range(4):
                    k = half * 4 + kk
                    P.op(pe, lambda: nc.tensor.transpose(pt[:, kk * 128:(kk + 1) * 128], h[:, k, b * 128:(b + 1) * 128], ident[:, :]), r=[hb_[k], ident], w=[pt])
                evac(big[:, b * 1024 + half * 512: b * 1024 + (half + 1) * 512], pt[:, :], [pt], [big])
        P.dma(pool, y_out[s, tok0:tok0 + 512, :].rearrange("(b p) n -> p b n", p=128), big[:, :].rearrange("p (b n) -> p b n", n=1024), r=[big], w=[db("Y", s, t)])

    prologue()
    P.barrier()
    segs = list(range(depth + 1))
    for seg in segs:
        for s in range(NSEQ):
            for t in range(ntiles):
                if seg > 0:
                    plan_attn(s, t, seg - 1)
                    plan_wout(seg - 1)
                    plan_ffn(2 * (seg - 1) + 1)
                if seg < depth:
                    plan_ffn(2 * seg)
                    plan_win(seg)
    for seg in segs:
        for s in range(NSEQ):
            for t in range(ntiles):
                if seg == 0:
                    load_x(s, t)
                else:
                    load_h(s, t)
                    attn(s, t, seg - 1)
                    post(s, t, seg - 1)
                if seg < depth:
                    pre(s, t, seg)
                    store_h(s, t)
                else:
                    final(s, t)
    P.barrier()
    es.close()
    return nc


_NC_CACHE = {}


def run(seqs, weights, S):
    NSEQ = seqs[0].shape[0]
    key = (NSEQ, S)
    if key not in _NC_CACHE:
        _NC_CACHE[key] = build(NSEQ, S)
    nc = _NC_CACHE[key]
    consts = host_consts(128 + S)
    in_maps = []
    for xs in seqs:
        m = {"x": np.ascontiguousarray(xs, dtype=np.float32)}
        m.update(weights)
        m.update(consts)
        in_maps.append(m)
    res = run_bass_kernel_spmd(nc, in_maps, core_ids=list(range(len(seqs))))
    return [r["y"] for r in res.results]


def kernel(**inputs):
    xp = np.asarray(inputs["x_prompt"], np.float32)
    xs = np.asarray(inputs["x_sample"], np.float32)
    allx = np.concatenate([xp, xs], axis=0)
    S = allx.shape[1]
    wnames = ["meta", "rel_bias", "g_ffn1", "w_ffn1_gu", "w_ffn1_down", "g_mix", "w_in", "diff_lambda", "g_subln", "sinks",
              "g_cq", "g_ckv", "w_uq", "w_ukv", "w_out", "g_ffn2", "w_ffn2_gu", "w_ffn2_down", "g_final"]
    weights = {k: np.ascontiguousarray(np.asarray(inputs[k], np.float32)) for k in wnames}
    assign = [(0, 1), (2, 3), (4, 5), (6, 7), (8, 9), (0, 1), (2, 3), (4, 5)]
    seqs = [allx[list(a)] for a in assign]
    ys = run(seqs, weights, S)
    out = np.empty_like(allx)
    for c in range(5):
        out[list(assign[c])] = ys[c]
    return (out[:xp.shape[0]].copy(), out[xp.shape[0]:].copy())
```

```python
import contextlib
import math
import numpy as np
import concourse.bass as bass
import concourse.mybir as mybir
from concourse.bass_utils import run_bass_kernel_spmd

F32 = mybir.dt.float32
BF16 = mybir.dt.bfloat16
ALU = mybir.AluOpType
AF = mybir.ActivationFunctionType
AX = mybir.AxisListType

D = 1024
KD = 8
DFF = 2816
NFC = 22
NMETA = 16
MS = 112
NEG = -30000.0
EPS = 1e-6
DEPTH = 2
SAME_SYNC = True
FULL_SYNC = False
NQS = 20


class Buf:
    __slots__ = ("w", "r")

    def __init__(self):
        self.w = {}
        self.r = {}


class Tl:
    def __init__(self, t):
        self.t = t
        self.b = Buf()

    def __getitem__(self, idx):
        return self.t[idx]


class Al:
    def __init__(self, fn):
        self.fn = fn
        self.b = Buf()

    def __getitem__(self, idx):
        return self.fn()[idx]


def alias_begin(parent, children):
    for c in children:
        for k, tv in list(parent.b.w.items()) + list(parent.b.r.items()):
            for d_ in (c.b.w,):
                if k not in d_ or d_[k][1] < tv[1]:
                    d_[k] = tv


def alias_end(parent, children):
    for c in children:
        for k, tv in list(c.b.w.items()) + list(c.b.r.items()):
            if k not in parent.b.r or parent.b.r[k][1] < tv[1]:
                parent.b.r[k] = tv


class Eng:
    def __init__(self, h, name, sem, is_pe=False):
        self.h = h
        self.name = name
        self.key = name
        self.sem = sem
        self.n = 0
        self.seen = {}
        self.is_pe = is_pe
        self.qsems = None
        self.qi = 0


class Prog:
    def __init__(self, nc, es):
        self.nc = nc
        self.es = es
        mk = lambda nm: nc.alloc_semaphore(nm)
        self.pe = Eng(nc.tensor, "pe", mk("c_pe"), True)
        self.act = Eng(nc.scalar, "act", mk("c_act"))
        self.dve = Eng(nc.vector, "dve", mk("c_dve"))
        self.pool = Eng(nc.gpsimd, "pool", mk("c_pool"))
        self.sp = Eng(nc.sync, "sp", mk("c_sp"))
        self.engs = [self.pe, self.act, self.dve, self.pool, self.sp]
        for e in (self.sp, self.pool, self.act):
            e.qsems = [mk(f"q_{e.name}_{i}") for i in range(NQS)]

    def _waits(self, E, need):
        for key, (sem, val) in need.items():
            if E.seen.get(key, 0) >= val:
                continue
            E.h.wait_ge(sem, val)
            E.seen[key] = val

    def _need(self, E, r, w):
        need = {}

        def merge(d, same_ok):
            for k, tv in d.items():
                if k == E.key and (E.is_pe or not SAME_SYNC or not (same_ok or FULL_SYNC)):
                    continue
                if k not in need or need[k][1] < tv[1]:
                    need[k] = tv
        for b in r:
            merge(b.w, True)
        for b in w:
            merge(b.w, False)
            merge(b.r, False)
        return need

    def _upd(self, key, tok, r, w):
        for b in r:
            b.r[key] = tok
        for b in w:
            b.w = {key: tok}
            b.r = {}

    def op(self, E, fn, r=(), w=()):
        r = [x if isinstance(x, Buf) else x.b for x in r]
        w = [x if isinstance(x, Buf) else x.b for x in w]
        self._waits(E, self._need(E, r, w))
        ins = fn()
        E.n += 1
        ins.then_inc(E.sem, 1)
        self._upd(E.key, (E.sem, E.n), r, w)

    def dma(self, E, out, in_, r=(), w=(), **kw):
        r = [x if isinstance(x, Buf) else x.b for x in r]
        w = [x if isinstance(x, Buf) else x.b for x in w]
        slot = E.qi % NQS
        rnd = E.qi // NQS
        E.qi += 1
        sem = E.qsems[slot]
        key = f"q_{E.name}_{slot}"
        need = self._need(E, r, w)
        if rnd > 0:
            need[key] = (sem, 16 * rnd)
        self._waits(E, need)
        E.h.dma_start(out=out, in_=in_, **kw).then_inc(sem, 16)
        self._upd(key, (sem, 16 * (rnd + 1)), r, w)

    def barrier(self):
        for E in self.engs:
            need = {}
            for X in self.engs:
                if X is not E and X.n > 0:
                    need[X.key] = (X.sem, X.n)
                if X.qsems is not None:
                    for slot in range(min(NQS, X.qi)):
                        cnt = (X.qi - 1 - slot) // NQS + 1
                        need[f"q_{X.name}_{slot}"] = (X.qsems[slot], 16 * cnt)
            self._waits(E, need)

    def sb(self, name, shape, dt):
        return Tl(self.es.enter_context(self.nc.sbuf_tensor(name, list(shape), dt)))

    def ps(self, name, shape, dt=F32):
        return Tl(self.es.enter_context(self.nc.psum_tensor(name, list(shape), dt)))


class Ring:
    def __init__(self, P, name, shape, dt, depth):
        self.P = P
        self.slots = [P.sb(f"{name}{i}", shape, dt) for i in range(depth)]
        self.depth = depth
        self.plan = []
        self.issued = 0
        self.consumed = 0

    def pop(self):
        lim = min(self.consumed + self.depth, len(self.plan))
        while self.issued < lim:
            self.plan[self.issued](self.slots[self.issued % self.depth])
            self.issued += 1
        s = self.slots[self.consumed % self.depth]
        self.consumed += 1
        return s


def t5_bucket_np(rel):
    rel = np.asarray(rel, np.int32)
    half = 16
    max_exact = 8
    ret = np.where(rel > 0, half, 0)
    n = np.abs(rel)
    nf = np.maximum(n, 1).astype(np.float32)
    try:
        import jax
        import jax.numpy as jnp
        with jax.default_device(jax.devices("cpu")[0]):
            lg = (jnp.log(jnp.asarray(nf) / max_exact) / math.log(128 / max_exact) * (half - max_exact)).astype(jnp.int32)
            lg = np.asarray(lg)
    except Exception:
        lg = (np.log(nf / np.float32(max_exact)) / np.float32(math.log(128 / max_exact)) * np.float32(half - max_exact)).astype(np.int32)
    large = np.minimum(max_exact + lg, half - 1)
    return (ret + np.where(n < max_exact, n, large)).astype(np.int32)


def host_consts(Lp):
    k = np.arange(128)[:, None]
    c = np.arange(384)[None, :]
    d = k - c + 128
    bk = t5_bucket_np(d).astype(np.float32)
    mb = (np.abs(d) <= 128).astype(np.float32)
    negb = np.where(np.abs(d) <= 128, 0.0, NEG).astype(np.float32)
    pos = (np.arange(Lp) - MS).astype(np.float32)
    inv = (10000.0 ** (-np.arange(0, 32, 2, dtype=np.float32) / np.float32(32))).astype(np.float32)
    ang = pos[:, None] * inv[None, :]
    ang = np.concatenate([ang, ang], axis=-1)
    cos = np.cos(ang).astype(np.float32).T.copy()
    sin = np.sin(ang).astype(np.float32).T.copy()
    ss = sin.copy()
    ss[:16] = -ss[:16]
    ident = np.eye(128, dtype=np.float32)
    return {"c_bk": bk, "c_mb": mb, "c_negb": negb, "c_cos": cos, "c_ss": ss, "c_ident": ident}


AQ, AK, AV, BQ, BK_, BV, CQ, CKV, KR = 0, 512, 1024, 1536, 1792, 1920, 2048, 2304, 2432
WIN_SLOTS = []
for _h in range(0, 4, 2):
    WIN_SLOTS.append([(0, AQ + _h * 128, 256)])
for _h in range(0, 4, 2):
    WIN_SLOTS.append([(0, AK + _h * 128, 256)])
WIN_SLOTS.append([(0, BQ, 256)])
WIN_SLOTS.append([(0, BK_, 64), (64, BK_, 64), (128, BK_ + 64, 64), (192, BK_ + 64, 64)])
WIN_SLOTS.append([(0, CQ, 256)])
WIN_SLOTS.append([(0, CKV, 128), (128, CKV, 64), (192, KR, 32), (224, KR, 32)])
WIN_SLOTS.append([(0, CKV, 64), (64, KR + 16, 16), (80, KR, 16), (96, KR + 16, 16), (112, KR, 16), (128, CKV, 128)])
WIN_SLOTS.append([(0, AV, 256)])
WIN_SLOTS.append([(0, AV + 256, 256)])
WIN_SLOTS.append([(0, BV, 128), (128, BV, 128)])
NWS = len(WIN_SLOTS)


def build(NSEQ, S, depth=DEPTH, stop_after=None):
    NT = S // 512
    NB = 1 + S // 128
    Lp = 128 + S
    KCH = 13 if NB % 13 == 0 else (5 if NB % 5 == 0 else 1)
    NCH = NB // KCH
    nc = bass.Bass("TRN2", target_bir_lowering=False)
    es = contextlib.ExitStack()
    P = Prog(nc, es)
    pe, act, dve, pool, sp = P.pe, P.act, P.dve, P.pool, P.sp

    def din(name, shape, dt=F32):
        return nc.dram_tensor(name, list(shape), dt, kind="ExternalInput").ap()

    def dscr(name, shape, dt):
        return nc.dram_tensor(name, list(shape), dt, kind="Internal").ap()

    x_in = din("x", [NSEQ, S, D])
    meta_in = din("meta", [NMETA, D])
    rb_in = din("rel_bias", [32, 8])
    g_ffn1 = din("g_ffn1", [depth, D]); w_f1gu = din("w_ffn1_gu", [depth, D, 2 * DFF]); w_f1d = din("w_ffn1_down", [depth, DFF, D])
    g_mix = din("g_mix", [depth, D]); w_in = din("w_in", [depth, D, 2464])
    dlam = din("diff_lambda", [depth, 4, 64]); g_subln = din("g_subln", [depth, 128]); sinks = din("sinks", [depth, 4])
    g_cq = din("g_cq", [depth, 256]); g_ckv = din("g_ckv", [depth, 128])
    w_uq = din("w_uq", [depth, 256, 384]); w_ukv = din("w_ukv", [depth, 128, 512]); w_out = din("w_out", [depth, D, D])
    g_ffn2 = din("g_ffn2", [depth, D]); w_f2gu = din("w_ffn2_gu", [depth, D, 2 * DFF]); w_f2d = din("w_ffn2_down", [depth, DFF, D])
    g_final = din("g_final", [D])
    c_bk = din("c_bk", [128, 384]); c_mb = din("c_mb", [128, 384]); c_negb = din("c_negb", [128, 384])
    c_cos = din("c_cos", [32, Lp]); c_ss = din("c_ss", [32, Lp]); c_ident = din("c_ident", [128, 128])
    y_out = nc.dram_tensor("y", [NSEQ, S, D], F32, kind="ExternalOutput").ap()

    GU = dscr("s_gu", [2 * depth, NFC, 128, 2048], BF16)
    DN = dscr("s_dn", [2 * depth, KD, 128, DFF], BF16)
    WIN = dscr("s_win", [depth, NWS, 128, 2048], BF16)
    WOUT = dscr("s_wout", [depth, KD, 128, 1536], BF16)
    QA = dscr("s_qa", [NSEQ, 2, 4, 128, Lp], BF16); KA = dscr("s_ka", [NSEQ, 2, 4, 128, Lp], BF16)
    VA = dscr("s_va", [NSEQ, 2, 4, 128, NB, 128], BF16)
    QB = dscr("s_qb", [NSEQ, 2, 2, 128, Lp], BF16); KB = dscr("s_kb", [NSEQ, 2, 2, 128, Lp], BF16)
    VB = dscr("s_vb", [NSEQ, 2, 128, NB, 130], BF16)
    QC = dscr("s_qc", [NSEQ, 2, 4, 128, Lp], BF16); KC = dscr("s_kc", [NSEQ, 2, 4, 128, Lp], BF16)
    VC = dscr("s_vc", [NSEQ, 2, 4, 128, NB, 65], BF16)
    HS = dscr("s_h", [NSEQ, KD, 128, Lp], F32)
    dbufs = {}

    def db(*key):
        if key not in dbufs:
            dbufs[key] = Buf()
        return dbufs[key]

    def tile_cols(t):
        return (0, 128) if t == 0 else (128 + (t - 1) * 512, 512)

    def tile_blocks(t):
        return (0, 1) if t == 0 else (1 + 4 * (t - 1), 4)

    ntiles = NT + 1

    ident = P.sb("ident", [128, 128], F32)
    ones_f = P.sb("ones_f", [128, 128], F32)
    sel65 = P.sb("sel65", [128, 64], F32)
    epsc = P.sb("epsc", [128, 1], F32)
    gcol = P.sb("gcol", [128, depth * 3 * KD + KD + depth * 4], F32)
    rbb = P.sb("rbb", [128, 256], F32)
    TA = P.sb("TA", [128, 4, 384], F32); TB = P.sb("TB", [128, 4, 384], F32)
    nlam = P.sb("nlam", [128, depth], F32); esink = P.sb("esink", [128, depth * 4], F32)
    wuq = P.sb("wuq", [128, depth, 2, 384], BF16); wuqs = P.sb("wuqs", [128, depth, 2, 384], BF16)
    wukv = P.sb("wukv", [128, depth, 512], BF16)

    def gc(kind, l):
        o = (l * 3 + kind) * KD
        return gcol[:, o:o + KD]
    GFIN = depth * 3 * KD
    GSM = GFIN + KD

    h = P.sb("h", [128, KD, 512], F32)
    xn = P.sb("xn", [128, KD, 512], BF16)
    aT = P.sb("aT", [128, NFC, 512], BF16)
    big = P.sb("big", [128, 4096], F32)
    sq = [P.sb(f"sq{i}", [128, 512], F32) for i in range(2)]
    rs = P.sb("rs", [128, 512], F32)
    sg = [P.sb(f"sg{i}", [128, 512], F32) for i in range(2)]
    cbk, cmb, cnegb = Al(lambda: sq[0][:, 0:384]), Al(lambda: sq[1][:, 0:384]), Al(lambda: sg[0][:, 0:384])
    ringG = Ring(P, "rg", [128, 2048], BF16, 3)
    ringD = Ring(P, "rd", [128, DFF], BF16, 3)
    ringK = Ring(P, "rk", [128, KCH * 128], BF16, 2)
    ringV = Ring(P, "rv", [128, KCH * 128], BF16, 2)
    qa = P.sb("qa", [128, 4, 512], BF16); qb = P.sb("qb", [128, 2, 512], BF16); qc = P.sb("qc", [128, 4, 512], BF16)
    kbn = P.sb("kbn", [128, 2, 768], BF16); vbn = P.sb("vbn", [128, 6, 130], BF16)
    cat = P.sb("cat", [128, 12, 512], BF16)
    acc2 = P.sb("acc2", [128, 2, 512], F32)
    ones_b = P.sb("ones_b", [128, 128], BF16)
    NE = 3
    Et = [P.sb(f"E{i}", [128, 2, 512], BF16) for i in range(NE)]
    Eh = [[Buf(), Buf()] for _ in range(NE)]
    tmpb = [P.sb(f"tmpb{i}", [128, 512], F32) for i in range(2)]
    t0 = P.sb("t0", [128, 512], F32); t1 = P.sb("t1", [128, 512], F32)
    rr = [P.sb(f"rr{i}", [128, 512], F32) for i in range(2)]
    osb = P.sb("osb", [128, 512], F32)
    stg = [Al(lambda: aT[:, 0:4, :]), Al(lambda: aT[:, 4:8, :])]
    vstg = Al(lambda: aT[:, 8:12, :])
    vbstg = P.sb("vbstg", [128, 4, 130], BF16)
    vcstg = P.sb("vcstg", [128, 4, 4, 65], BF16)
    cqn = P.sb("cqn", [128, 2, 512], BF16); ckvn = P.sb("ckvn", [128, 512], BF16)
    ropec = P.sb("ropec", [128, 2, 512], F32)
    krope = P.sb("krope", [128, 512], BF16)
    pb = [P.ps(f"pb{i}", [128, 512]) for i in range(8)]
    xnb = [Buf() for _ in range(KD)]
    aTb = [Buf() for _ in range(NFC)]
    hb_ = [Buf() for _ in range(KD)]

    def mm(out, lhsT, rhs, start, stop, r, w):
        P.op(pe, lambda: nc.tensor.matmul(out, lhsT, rhs, start=start, stop=stop), r=r, w=w)

    def recip(out, in_, scratch, r, w, bias=None):
        w1 = [x for x in w if x is not None][:1]
        P.op(act, lambda: nc.scalar.activation(out=out, in_=in_, func=AF.Ln, **({"bias": bias} if bias is not None else {})), r=r, w=w1)
        P.op(act, lambda: nc.scalar.activation(out=out, in_=out, func=AF.Exp, scale=-1.0), r=w1, w=w1)

    def rms_scale(srcs, D_, W, r):
        n = len(srcs)
        for i, sap in enumerate(srcs):
            s_ = sq[i % 2]
            P.op(act, lambda: nc.scalar.activation(out=s_[:, :W], in_=sap, func=AF.Square), r=(r[i] if isinstance(r[0], list) else r), w=[s_])
            mm(pb[7][:, :W], ones_f[:, :], s_[:, :W], i == 0, i == n - 1, [s_, ones_f], [pb[7]])
        P.op(act, lambda: nc.scalar.activation(out=rs[:, :W], in_=pb[7][:, :W], func=AF.Ln, scale=1.0 / D_, bias=epsc[:, 0:1]), r=[pb[7], epsc], w=[rs])
        P.op(act, lambda: nc.scalar.activation(out=rs[:, :W], in_=rs[:, :W], func=AF.Exp, scale=-0.5), r=[rs], w=[rs])

    def rmsnorm_h(gap, W):
        rms_scale([h[:, k, :W] for k in range(KD)], D, W, [[hb_[k]] for k in range(KD)])
        for k in range(KD):
            P.op(dve, lambda: nc.vector.scalar_tensor_tensor(out=xn[:, k, :W], in0=h[:, k, :W], scalar=gap[:, k:k + 1],
                                                             in1=rs[:, :W], op0=ALU.mult, op1=ALU.mult), r=[hb_[k], rs, gcol], w=[xnb[k]])

    def ffn(f, gap, W):
        rmsnorm_h(gap, W)
        for c in range(NFC):
            sl = ringG.pop()
            par = c % 2
            pg, pu = pb[2 * par], pb[2 * par + 1]
            for part, pt in ((0, pg), (1, pu)):
                for k in range(KD):
                    mm(pt[:, :W], sl[:, k * 256 + part * 128:k * 256 + part * 128 + 128], xn[:, k, :W], k == 0, k == KD - 1, [sl, xnb[k]], [pt])
            s_ = sg[par]
            P.op(act, lambda: nc.scalar.activation(out=s_[:, :W], in_=pg[:, :W], func=AF.Silu), r=[pg], w=[s_])
            P.op(dve, lambda: nc.vector.tensor_tensor(out=aT[:, c, :W], in0=s_[:, :W], in1=pu[:, :W], op=ALU.mult), r=[s_, pu], w=[aTb[c], aT])
        for d in range(KD):
            sl = ringD.pop()
            po = pb[4 + d % 2]
            for c in range(NFC):
                mm(po[:, :W], sl[:, c * 128:(c + 1) * 128], aT[:, c, :W], c == 0, c == NFC - 1, [sl, aTb[c]], [po])
            P.op(dve, lambda: nc.vector.scalar_tensor_tensor(out=h[:, d, :W], in0=po[:, :W], scalar=0.5, in1=h[:, d, :W],
                                                             op0=ALU.mult, op1=ALU.add), r=[po, hb_[d]], w=[hb_[d]])

    def plan_ffn(f):
        for c in range(NFC):
            ringG.plan.append(lambda sl, c=c: P.dma(sp, sl[:, :], GU[f, c], r=[db("GU", f)], w=[sl]))
        for d in range(KD):
            ringD.plan.append(lambda sl, d=d: P.dma(sp, sl[:, :], DN[f, d], r=[db("DN", f)], w=[sl]))

    def plan_win(l):
        for s_ in range(NWS):
            ringG.plan.append(lambda sl, s_=s_: P.dma(sp, sl[:, :], WIN[l, s_], r=[db("WIN", l)], w=[sl]))

    def plan_wout(l):
        for d in range(KD):
            ringD.plan.append(lambda sl, d=d: P.dma(sp, sl[:, 0:1536], WOUT[l, d], r=[db("WOUT", l)], w=[sl]))

    evac_i = [0]

    def evac(out, in_, r, w):
        evac_i[0] += 1
        if evac_i[0] % 2 == 0:
            P.op(act, lambda: nc.scalar.activation(out=out, in_=in_, func=AF.Copy), r=r, w=w)
        else:
            P.op(dve, lambda: nc.vector.tensor_copy(out=out, in_=in_), r=r, w=w)

    def prologue():
        P.dma(sp, ident[:, :], c_ident, w=[ident])
        P.dma(sp, cbk[:, :], c_bk, w=[cbk, sq[0]]); P.dma(sp, cmb[:, :], c_mb, w=[cmb, sq[1]]); P.dma(sp, cnegb[:, :], c_negb, w=[cnegb, sg[0]])
        P.op(dve, lambda: nc.vector.memset(ones_f[:, :], 1.0), w=[ones_f])
        P.op(dve, lambda: nc.vector.memset(epsc[:, :], EPS), w=[epsc])
        P.op(dve, lambda: nc.vector.memset(ones_b[:, :], 1.0), w=[ones_b])
        P.op(dve, lambda: nc.vector.memset(sel65[:, :], 0.0), w=[sel65])
        P.op(dve, lambda: nc.vector.memset(sel65[64:65, :], 1.0), w=[sel65])
        P.dma(sp, rbb[:, :], rb_in.rearrange("b h -> (b h)").partition_broadcast(128), w=[rbb])
        def gload(dst, src1d, k):
            P.dma(sp, dst, src1d.rearrange("(k p) -> p k", p=128), w=[gcol], allow_slow_non_contiguous=True)
        for l in range(depth):
            gload(gc(0, l), g_ffn1[l], KD); gload(gc(1, l), g_mix[l], KD); gload(gc(2, l), g_ffn2[l], KD)
            o = GSM + 4 * l
            gload(gcol[:, o:o + 2], g_cq[l], 2); gload(gcol[:, o + 2:o + 3], g_ckv[l], 1); gload(gcol[:, o + 3:o + 4], g_subln[l], 1)
        gload(gcol[:, GFIN:GFIN + KD], g_final, KD)
        dl = big
        for l in range(depth):
            P.dma(sp, big[:, 0:256], dlam[l].rearrange("a b -> (a b)").partition_broadcast(128), w=[big])
            P.op(dve, lambda: nc.vector.tensor_tensor(out=big[:, 256:320], in0=big[:, 0:64], in1=big[:, 64:128], op=ALU.mult), r=[big], w=[big])
            P.op(dve, lambda: nc.vector.tensor_tensor(out=big[:, 320:384], in0=big[:, 128:192], in1=big[:, 192:256], op=ALU.mult), r=[big], w=[big])
            P.op(dve, lambda: nc.vector.tensor_reduce(out=big[:, 384:385], in_=big[:, 256:320], axis=AX.X, op=ALU.add), r=[big], w=[big])
            P.op(dve, lambda: nc.vector.tensor_reduce(out=big[:, 385:386], in_=big[:, 320:384], axis=AX.X, op=ALU.add), r=[big], w=[big])
            P.op(act, lambda: nc.scalar.activation(out=big[:, 386:388], in_=big[:, 384:386], func=AF.Exp), r=[big], w=[big])
            lam_init = 0.8 - 0.6 * math.exp(-0.3 * l)
            P.op(dve, lambda: nc.vector.tensor_tensor(out=big[:, 388:389], in0=big[:, 387:388], in1=big[:, 386:387], op=ALU.subtract), r=[big], w=[big])
            P.op(dve, lambda: nc.vector.tensor_scalar(out=nlam[:, l:l + 1], in0=big[:, 388:389], scalar1=-lam_init, scalar2=None, op0=ALU.add), r=[big], w=[nlam])
            P.dma(sp, big[:, 400:404], sinks[l].partition_broadcast(128), w=[big])
            P.op(act, lambda: nc.scalar.activation(out=esink[:, 4 * l:4 * l + 4], in_=big[:, 400:404], func=AF.Exp), r=[big], w=[esink])
        m_ = tmpb[0]
        for hh in range(8):
            dst = TA[:, hh, :] if hh < 4 else TB[:, hh - 4, :]
            dstb = TA if hh < 4 else TB
            for b in range(32):
                o_ = dst if b == 0 else m_[:, 0:384]
                P.op(dve, lambda: nc.vector.tensor_scalar(out=o_, in0=cbk[:, :], scalar1=float(b), scalar2=rbb[:, b * 8 + hh:b * 8 + hh + 1],
                                                          op0=ALU.is_equal, op1=ALU.mult), r=[cbk, rbb], w=[dstb if b == 0 else m_])
                if b > 0:
                    P.op(dve, lambda: nc.vector.tensor_tensor(out=dst, in0=dst, in1=m_[:, 0:384], op=ALU.add), r=[m_, dstb], w=[dstb])
            if hh >= 4:
                P.op(dve, lambda: nc.vector.tensor_tensor(out=dst, in0=dst, in1=cmb[:, :], op=ALU.mult), r=[cmb, dstb], w=[dstb])
                P.op(dve, lambda: nc.vector.tensor_tensor(out=dst, in0=dst, in1=cnegb[:, :], op=ALU.add), r=[cnegb, dstb], w=[dstb])
        for l in range(depth):
            P.dma(pool, wuq[:, l, :, :], w_uq[l].rearrange("(kc p) n -> p kc n", p=128), w=[wuq])
            P.dma(pool, wukv[:, l, :], w_ukv[l], w=[wukv])
            for hh in range(4):
                src = w_uq[l].rearrange("(kc p) n -> p kc n", p=128)
                o = hh * 96
                P.dma(pool, wuqs[:, l, :, o:o + 64], src[:, :, o:o + 64], w=[wuqs])
                P.dma(pool, wuqs[:, l, :, o + 64:o + 80], src[:, :, o + 80:o + 96], w=[wuqs])
                P.dma(pool, wuqs[:, l, :, o + 80:o + 96], src[:, :, o + 64:o + 80], w=[wuqs])
        P.barrier()
        srct = [Al(lambda: big[:, 0:2816]), Al(lambda: h[:, :, :].rearrange("p k n -> p (k n)")[:, 0:2816])]
        dstt = [Al(lambda: aT[:, :, :].rearrange("p c n -> p (c n)")[:, 0:2816]), Al(lambda: aT[:, :, :].rearrange("p c n -> p (c n)")[:, 2816:5632])]
        cnt = [0]
        cast_engs = [dve, act, pool]

        def conv(pieces, n, dst_ap, wbuf):
            i = cnt[0] % 2
            cnt[0] += 1
            for (view_fn, src_ap) in pieces:
                P.dma(sp, view_fn(srct[i]), src_ap, w=[srct[i]])
            E = cast_engs[cnt[0] % 3]
            if E is act:
                P.op(act, lambda: nc.scalar.activation(out=dstt[i][:, 0:n], in_=srct[i][:, 0:n], func=AF.Copy), r=[srct[i]], w=[dstt[i]])
            elif E is dve:
                P.op(dve, lambda: nc.vector.tensor_copy(out=dstt[i][:, 0:n], in_=srct[i][:, 0:n]), r=[srct[i]], w=[dstt[i]])
            else:
                P.op(pool, lambda: nc.gpsimd.tensor_copy(out=dstt[i][:, 0:n], in_=srct[i][:, 0:n]), r=[srct[i]], w=[dstt[i]])
            P.dma(act, dst_ap, dstt[i][:, 0:n], r=[dstt[i]], w=[wbuf])

        for l in range(depth):
            for fi, (wgu, wdn) in enumerate(((w_f1gu, w_f1d), (w_f2gu, w_f2d))):
                f = 2 * l + fi
                for c in range(NFC):
                    pcs = []
                    for part in range(2):
                        src = wgu[l][:, part * DFF + c * 128: part * DFF + (c + 1) * 128].rearrange("(k p) j -> p k j", p=128)
                        pcs.append((lambda sb_, part=part: sb_[:, 0:2048].rearrange("p (k j) -> p k j", j=256)[:, :, part * 128:(part + 1) * 128], src))
                    conv(pcs, 2048, GU[f, c], db("GU", f))
                for d in range(KD):
                    src = wdn[l][:, d * 128:(d + 1) * 128].rearrange("(c p) j -> p c j", p=128)
                    conv([(lambda sb_: sb_[:, 0:DFF].rearrange("p (c j) -> p c j", j=128), src)], DFF, DN[f, d], db("DN", f))
            for s_, pieces in enumerate(WIN_SLOTS):
                pcs = []
                for (do, so, n) in pieces:
                    src = w_in[l][:, so:so + n].rearrange("(k p) j -> p k j", p=128)
                    pcs.append((lambda sb_, do=do, n=n: sb_[:, 0:2048].rearrange("p (k j) -> p k j", j=256)[:, :, do:do + n], src))
                conv(pcs, 2048, WIN[l, s_], db("WIN", l))
            for d in range(KD):
                pcs = [(lambda sb_: sb_[:, 0:512].rearrange("p (c j) -> p c j", j=128),
                        w_out[l][0:512, d * 128:(d + 1) * 128].rearrange("(c p) j -> p c j", p=128)),
                       (lambda sb_: sb_[0:64, 512:1536].rearrange("p (c j) -> p c j", j=128),
                        w_out[l][512:1024, d * 128:(d + 1) * 128].rearrange("(c p) j -> p c j", p=64))]
                conv(pcs, 1536, WOUT[l, d], db("WOUT", l))
        P.op(dve, lambda: nc.vector.memset(vbstg[:, :, :], 1.0), w=[vbstg])
        P.op(dve, lambda: nc.vector.memset(vcstg[:, :, :, :], 1.0), w=[vcstg])
        P.op(dve, lambda: nc.vector.memset(cat[:, :, :], 0.0), w=[cat])
        P.op(dve, lambda: nc.vector.memset(qc[:, :, :], 0.0), w=[qc])

    def load_x(s, t):
        c0, W = tile_cols(t)
        xin = big
        if t == 0:
            P.op(dve, lambda: nc.vector.memset(big[:, 0:1024], 0.0), w=[big])
            P.dma(sp, big[MS:128, 0:1024], meta_in, w=[big])
            nb = 1
        else:
            tok0 = (t - 1) * 512
            P.dma(sp, big[:, :].rearrange("p (b n) -> p b n", n=1024), x_in[s, tok0:tok0 + 512, :].rearrange("(b p) n -> p b n", p=128), w=[big])
            nb = 4
        for k in range(KD):
            pt = pb[k % 2]
            for b in range(nb):
                P.op(pe, lambda: nc.tensor.transpose(pt[:, b * 128:(b + 1) * 128], big[:, b * 1024 + k * 128: b * 1024 + (k + 1) * 128], ident[:, :]),
                     r=[big, ident], w=[pt])
            evac(h[:, k, :W], pt[:, :W], [pt], [hb_[k]])

    def load_h(s, t):
        c0, W = tile_cols(t)
        P.dma(sp, h[:, :, :W], HS[s, :, :, c0:c0 + W].rearrange("k p n -> p k n"), r=[db("H", s, t)], w=hb_)

    def store_h(s, t):
        c0, W = tile_cols(t)
        P.dma(pool, HS[s, :, :, c0:c0 + W].rearrange("k p n -> p k n"), h[:, :, :W], r=hb_, w=[db("H", s, t)])

    def pre(s, t, l):
        c0, W = tile_cols(t)
        b0, nb = tile_blocks(t)
        bq_ = l % 2
        ffn(2 * l, gc(0, l), W)
        rmsnorm_h(gc(1, l), W)
        P.dma(sp, ropec[64:96, 0, :W], c_cos[:, c0:c0 + W], w=[ropec])
        P.dma(sp, ropec[64:96, 1, :W], c_ss[:, c0:c0 + W], w=[ropec])
        pi = [0]
        alias_begin(aT, stg + [vstg])

        def proj(sl, half):
            pt = pb[pi[0] % 2]
            pi[0] += 1
            for k in range(KD):
                mm(pt[:, :W], sl[:, k * 256 + half * 128:k * 256 + half * 128 + 128], xn[:, k, :W], k == 0, k == KD - 1, [sl, xnb[k]], [pt])
            return pt
        for kind, dst in ((0, QA), (1, KA)):
            st = stg[kind]
            for i in range(2):
                sl = ringG.pop()
                for half in range(2):
                    pt = proj(sl, half)
                    evac(st[:, 2 * i + half, :W], pt[:, :W], [pt], [st])
            P.dma(pool, dst[s, bq_, :, :, c0:c0 + W].rearrange("h p n -> p h n"), st[:, :, :W], r=[st], w=[db("QA" if kind == 0 else "KA", s, bq_, t)])
        for kind, dst in ((0, QB), (1, KB)):
            st = stg[kind]
            sl = ringG.pop()
            for half in range(2):
                pt = proj(sl, half)
                evac(st[:, half, :W], pt[:, :W], [pt], [st])
            P.dma(pool, dst[s, bq_, :, :, c0:c0 + W].rearrange("h p n -> p h n"), st[:, 0:2, :W], r=[st], w=[db("QB" if kind == 0 else "KB", s, bq_, t)])
        sl = ringG.pop()
        for half in range(2):
            for k in range(KD):
                mm(pb[2 + half][:, :W], sl[:, k * 256 + half * 128:k * 256 + half * 128 + 128], xn[:, k, :W], k == 0, k == KD - 1, [sl, xnb[k]], [pb[2 + half]])
        sl = ringG.pop()
        for half in range(2):
            for k in range(KD):
                mm(pb[4 + half][:, :W], sl[:, k * 256 + half * 128:k * 256 + half * 128 + 128], xn[:, k, :W], k == 0, k == KD - 1, [sl, xnb[k]], [pb[4 + half]])
        sl = ringG.pop()
        for k in range(KD):
            mm(pb[6][:, :W], sl[:, k * 256:k * 256 + 128], xn[:, k, :W], k == 0, k == KD - 1, [sl, xnb[k]], [pb[6]])
        go = GSM + 4 * l
        rms_scale([pb[2][:, :W], pb[3][:, :W]], 256, W, [pb[2], pb[3]])
        for i in range(2):
            P.op(dve, lambda: nc.vector.scalar_tensor_tensor(out=cqn[:, i, :W], in0=pb[2 + i][:, :W], scalar=gcol[:, go + i:go + i + 1],
                                                             in1=rs[:, :W], op0=ALU.mult, op1=ALU.mult), r=[pb[2 + i], rs, gcol], w=[cqn])
        rms_scale([pb[4][:, :W]], 128, W, [pb[4]])
        P.op(dve, lambda: nc.vector.scalar_tensor_tensor(out=ckvn[:, :W], in0=pb[4][:, :W], scalar=gcol[:, go + 2:go + 3],
                                                         in1=rs[:, :W], op0=ALU.mult, op1=ALU.mult), r=[pb[4], rs, gcol], w=[ckvn])
        P.op(dve, lambda: nc.vector.tensor_tensor(out=t0[64:96, :W], in0=pb[5][64:96, :W], in1=ropec[64:96, 0, :W], op=ALU.mult), r=[pb[5], ropec], w=[t0])
        P.op(dve, lambda: nc.vector.tensor_tensor(out=t1[64:96, :W], in0=pb[6][64:96, :W], in1=ropec[64:96, 1, :W], op=ALU.mult), r=[pb[6], ropec], w=[t1])
        P.op(dve, lambda: nc.vector.tensor_tensor(out=krope[64:96, :W], in0=t0[64:96, :W], in1=t1[64:96, :W], op=ALU.add), r=[t0, t1], w=[krope])
        st = stg[1]
        for hh in range(4):
            pt = pb[hh % 2]
            mm(pt[0:64, :W], wukv[:, l, hh * 128:hh * 128 + 64], ckvn[:, :W], True, True, [wukv, ckvn], [pt])
            evac(st[0:64, hh, :W], pt[0:64, :W], [pt], [st])
            P.op(pool, lambda: nc.gpsimd.tensor_copy(out=st[64:96, hh, :W], in_=krope[64:96, :W]), r=[krope], w=[st])
        P.dma(pool, KC[s, bq_, :, 0:96, c0:c0 + W].rearrange("h p n -> p h n"), st[0:96, :, :W], r=[st], w=[db("KC", s, bq_, t)])
        st = stg[0]
        for hh in range(4):
            pa_, pb_ = pb[2], pb[3]
            for kc in range(2):
                mm(pa_[0:96, :W], wuq[:, l, kc, hh * 96:(hh + 1) * 96], cqn[:, kc, :W], kc == 0, kc == 1, [wuq, cqn], [pa_])
            for kc in range(2):
                mm(pb_[0:96, :W], wuqs[:, l, kc, hh * 96:(hh + 1) * 96], cqn[:, kc, :W], kc == 0, kc == 1, [wuqs, cqn], [pb_])
            evac(st[0:64, hh, :W], pa_[0:64, :W], [pa_], [st])
            P.op(dve, lambda: nc.vector.tensor_tensor(out=t0[64:96, :W], in0=pa_[64:96, :W], in1=ropec[64:96, 0, :W], op=ALU.mult), r=[pa_, ropec], w=[t0])
            P.op(dve, lambda: nc.vector.tensor_tensor(out=t1[64:96, :W], in0=pb_[64:96, :W], in1=ropec[64:96, 1, :W], op=ALU.mult), r=[pb_, ropec], w=[t1])
            P.op(dve, lambda: nc.vector.tensor_tensor(out=st[64:96, hh, :W], in0=t0[64:96, :W], in1=t1[64:96, :W], op=ALU.add), r=[t0, t1], w=[st])
        P.dma(pool, QC[s, bq_, :, 0:96, c0:c0 + W].rearrange("h p n -> p h n"), st[0:96, :, :W], r=[st], w=[db("QC", s, bq_, t)])
        for b in range(nb):
            pt = pb[b % 2]
            mm(pt[:, 0:256], ckvn[:, b * 128:(b + 1) * 128], wukv[:, l, :].rearrange("p (h j) -> p h j", j=128)[:, :, 64:128], True, True, [wukv, ckvn], [pt])
            evac(vcstg[:, b, :, 0:64], pt[:, 0:256].rearrange("p (h j) -> p h j", j=64), [pt], [vcstg])
        for hh in range(4):
            P.dma(pool, VC[s, bq_, hh, :, b0:b0 + nb, :], vcstg[:, 0:nb, hh, :], r=[vcstg], w=[db("VC", s, bq_, t)])
        for half in range(2):
            sl = ringG.pop()
            for b in range(nb):
                pt = pb[b % 2]
                for k in range(KD):
                    mm(pt[:, 0:256], xn[:, k, b * 128:(b + 1) * 128], sl[:, k * 256:(k + 1) * 256], k == 0, k == KD - 1, [sl, xnb[k]], [pt])
                evac(vstg[:, b, half * 256:(half + 1) * 256], pt[:, 0:256], [pt], [vstg])
        for hh in range(4):
            P.dma(pool, VA[s, bq_, hh, :, b0:b0 + nb, :], vstg[:, 0:nb, hh * 128:(hh + 1) * 128], r=[vstg], w=[db("VA", s, bq_, t)])
        sl = ringG.pop()
        for b in range(nb):
            pt = pb[b % 2]
            for k in range(KD):
                mm(pt[:, 0:128], xn[:, k, b * 128:(b + 1) * 128], sl[:, k * 256:k * 256 + 128], k == 0, k == KD - 1, [sl, xnb[k]], [pt])
            evac(vbstg[:, b, :].rearrange("p (g j) -> p g j", j=65)[:, :, 0:64], pt[:, 0:128].rearrange("p (g j) -> p g j", j=64), [pt], [vbstg])
        P.dma(pool, VB[s, bq_, :, b0:b0 + nb, :], vbstg[:, 0:nb, :], r=[vbstg], w=[db("VB", s, bq_, t)])
        alias_end(aT, stg + [vstg])

    def plan_attn(s, t, l):
        bq_ = l % 2
        allk = lambda kind: [db(kind, s, bq_, tt) for tt in range(ntiles)]
        for (kd, vd, kn, vn, vw) in ((KA, VA, "KA", "VA", 128), (KC, VC, "KC", "VC", 65)):
            for hh in range(4):
                for ch in range(NCH):
                    ringK.plan.append(lambda sl, kd=kd, kn=kn, hh=hh, ch=ch: P.dma(
                        sp, sl[:, :], kd[s, bq_, hh, :, ch * KCH * 128:(ch + 1) * KCH * 128], r=allk(kn), w=[sl]))
                    ringV.plan.append(lambda sl, vd=vd, vn=vn, hh=hh, ch=ch, vw=vw: P.dma(
                        sp, sl[:, 0:KCH * vw].rearrange("p (b j) -> p b j", j=vw), vd[s, bq_, hh, :, ch * KCH:(ch + 1) * KCH, :], r=allk(vn), w=[sl]))

    def attn(s, t, l):
        c0, W = tile_cols(t)
        b0, nb = tile_blocks(t)
        bq_ = l % 2
        P.dma(sp, qa[:, :, :W], QA[s, bq_, :, :, c0:c0 + W].rearrange("h p n -> p h n"), r=[db("QA", s, bq_, t)], w=[qa])
        P.dma(sp, qb[:, :, :W], QB[s, bq_, :, :, c0:c0 + W].rearrange("h p n -> p h n"), r=[db("QB", s, bq_, t)], w=[qb])
        P.dma(sp, qc[0:96, :, :W], QC[s, bq_, :, 0:96, c0:c0 + W].rearrange("h p n -> p h n"), r=[db("QC", s, bq_, t)], w=[qc])
        go = GSM + 4 * l
        lam_init = 0.8 - 0.6 * math.exp(-0.3 * l)
        ei = [0, 0]
        tbi = [0]

        def bias_exp(S_, Eb, Eap, j, kind, hh, scale):
            if kind == "C":
                if j == 0:
                    tb = tmpb[tbi[0] % 2]
                    tbi[0] += 1
                    P.op(dve, lambda: nc.vector.tensor_scalar(out=tb[:, :W], in0=S_[:, :W], scalar1=scale, scalar2=None, op0=ALU.mult),
                         r=[S_], w=[tb])
                    P.op(dve, lambda: nc.vector.memset(tb[0:MS, :W], NEG), w=[tb])
                    P.op(act, lambda: nc.scalar.activation(out=Eap, in_=tb[:, :W], func=AF.Exp), r=[tb], w=[Eb])
                else:
                    P.op(act, lambda: nc.scalar.activation(out=Eap, in_=S_[:, :W], func=AF.Exp, scale=scale), r=[S_], w=[Eb])
                return
            near = (j == 0) or (b0 - 1 <= j <= b0 + nb)
            if not near:
                col = rbb[:, 15 * 8 + hh:15 * 8 + hh + 1] if j < b0 else rbb[:, 31 * 8 + hh:31 * 8 + hh + 1]
                P.op(act, lambda: nc.scalar.activation(out=Eap, in_=S_[:, :W], func=AF.Exp, bias=col, scale=scale), r=[S_, rbb], w=[Eb])
                return
            tb = tmpb[tbi[0] % 2]
            tbi[0] += 1
            for qi in range(nb):
                dl_ = j - (b0 + qi)
                cs = slice(qi * 128, (qi + 1) * 128)
                if -1 <= dl_ <= 1:
                    tab = TA[:, hh, (1 - dl_) * 128:(2 - dl_) * 128]
                    P.op(dve, lambda: nc.vector.scalar_tensor_tensor(out=tb[:, cs], in0=S_[:, cs], scalar=scale, in1=tab, op0=ALU.mult, op1=ALU.add),
                         r=[S_, TA], w=[tb])
                else:
                    col = rbb[:, 15 * 8 + hh:15 * 8 + hh + 1] if dl_ < 0 else rbb[:, 31 * 8 + hh:31 * 8 + hh + 1]
                    P.op(dve, lambda: nc.vector.tensor_scalar(out=tb[:, cs], in0=S_[:, cs], scalar1=scale, scalar2=col, op0=ALU.mult, op1=ALU.add),
                         r=[S_, rbb], w=[tb])
            if j == 0:
                P.op(dve, lambda: nc.vector.memset(tb[0:MS, :W], NEG), w=[tb])
            P.op(act, lambda: nc.scalar.activation(out=Eap, in_=tb[:, :W], func=AF.Exp), r=[tb], w=[Eb])

        scaleA = 64 ** -0.5
        units = [(ch, jj) for ch in range(NCH) for jj in range(KCH)]
        n = len(units)
        pend = {"p1": None, "p2": None}

        def run_p1():
            if pend["p1"] is not None:
                f = pend["p1"]
                pend["p1"] = None
                f()

        def run_p2():
            run_p1()
            if pend["p2"] is not None:
                f = pend["p2"]
                pend["p2"] = None
                f()

        def a_fin1(hh, npool):
            O = [pb[4], pb[5]]
            P.op(act, lambda: nc.scalar.activation(out=t0[:, :W], in_=O[0][:, :W], func=AF.Copy), r=[O[0]], w=[t0])
            P.op(dve, lambda: nc.vector.tensor_copy(out=t1[:, :W], in_=O[1][:, :W]), r=[O[1]], w=[t1])
            for m in range(2):
                mm(pb[6 + m][:, :W], ones_f[:, :], acc2[:, m, :W], npool == 0, True, [ones_f, acc2], [pb[6 + m]])
                recip(rr[m][:, :W], pb[6 + m][:, :W], None, [pb[6 + m]], [rr[m]])
            P.op(dve, lambda: nc.vector.tensor_tensor(out=t0[:, :W], in0=t0[:, :W], in1=rr[0][:, :W], op=ALU.mult), r=[t0, rr[0]], w=[t0])
            P.op(dve, lambda: nc.vector.tensor_tensor(out=t1[:, :W], in0=t1[:, :W], in1=rr[1][:, :W], op=ALU.mult), r=[t1, rr[1]], w=[t1])
            P.op(dve, lambda: nc.vector.scalar_tensor_tensor(out=osb[:, :W], in0=t1[:, :W], scalar=nlam[:, l:l + 1], in1=t0[:, :W], op0=ALU.mult, op1=ALU.add),
                 r=[t0, t1, nlam], w=[osb])

        def a_fin2(hh):
            rms_scale([osb[:, :W]], 128, W, [osb])
            P.op(dve, lambda: nc.vector.tensor_scalar(out=rs[:, :W], in0=rs[:, :W], scalar1=(1.0 - lam_init), scalar2=None, op0=ALU.mult), r=[rs], w=[rs])
            P.op(dve, lambda: nc.vector.scalar_tensor_tensor(out=cat[:, hh, :W], in0=osb[:, :W], scalar=gcol[:, go + 3:go + 4], in1=rs[:, :W],
                                                             op0=ALU.mult, op1=ALU.mult), r=[osb, rs, gcol], w=[cat])

        def c_fin1(hh):
            evac(osb[0:65, :W], pb[4][0:65, :W], [pb[4]], [osb])

        def c_fin2(hh):
            mm(pb[6][0:64, :W], sel65[0:65, :], osb[0:65, :W], True, True, [sel65, osb], [pb[6]])
            recip(rr[0][0:64, :W], pb[6][0:64, :W], None, [pb[6]], [rr[0]])
            P.op(dve, lambda: nc.vector.tensor_tensor(out=cat[0:64, 8 + hh, :W], in0=osb[0:64, :W], in1=rr[0][0:64, :W], op=ALU.mult), r=[osb, rr[0]], w=[cat])

        P2A = min(6, n - 1)
        for hh in range(4):
            O = [pb[4], pb[5]]
            slots = {}
            vslots = {}

            def getslot(ch):
                if ch not in slots:
                    slots[ch] = ringK.pop()
                return slots[ch]

            def getv(ch):
                if ch not in vslots:
                    vslots[ch] = ringV.pop()
                return vslots[ch]

            def qk(i):
                ch, jj = units[i]
                ks = getslot(ch)
                par = i % 2
                for m in range(2):
                    S_ = pb[2 * par + m]
                    mm(S_[:, :W], ks[m * 64:(m + 1) * 64, jj * 128:(jj + 1) * 128], qa[m * 64:(m + 1) * 64, hh, :W], True, True, [ks, qa], [S_])
            qk(0)
            run_p1()
            npool = 0
            ndve = 0
            for i in range(n):
                if i == P2A:
                    run_p2()
                if i + 1 < n:
                    qk(i + 1)
                ch, jj = units[i]
                j = ch * KCH + jj
                vs = getv(ch)
                par = i % 2
                Ep = Et[ei[0] % NE]
                Eb2 = Eh[ei[0] % NE]
                ei[0] += 1
                for m in range(2):
                    S_ = pb[2 * par + m]
                    bias_exp(S_, Eb2[m], Ep[:, m, :W], j, "A", hh, scaleA)
                    mm(O[m][:, :W], vs[:, jj * 128:(jj + 1) * 128], Ep[:, m, :W], i == 0, i == n - 1, [vs, Eb2[m]], [O[m]])
                if i % 3 == 2 and i >= 8:
                    for m in range(2):
                        mm(pb[6 + m][:, :W], ones_b[:, :], Ep[:, m, :W], npool == 0, False, [ones_b, Eb2[m]], [pb[6 + m]])
                    npool += 1
                else:
                    if ndve == 0:
                        P.op(dve, lambda: nc.vector.tensor_copy(out=acc2[:, :, :W], in_=Ep[:, :, :W]), r=Eb2, w=[acc2])
                    else:
                        P.op(dve, lambda: nc.vector.tensor_tensor(out=acc2[:, :, :W], in0=acc2[:, :, :W], in1=Ep[:, :, :W], op=ALU.add), r=Eb2 + [acc2], w=[acc2])
                    ndve += 1
            run_p2()
            pend["p1"] = (lambda hh=hh, npool=npool: a_fin1(hh, npool))
            pend["p2"] = (lambda hh=hh: a_fin2(hh))
        scaleC = 96 ** -0.5
        LA = 2
        P2C = min(10, n - 1)
        for hh in range(4):
            O = pb[4]
            slots = {}
            vslots = {}

            def getslot(ch):
                if ch not in slots:
                    slots[ch] = ringK.pop()
                return slots[ch]

            def getv(ch):
                if ch not in vslots:
                    vslots[ch] = ringV.pop()
                return vslots[ch]

            def qk(i):
                ch, jj = units[i]
                ks = getslot(ch)
                S_ = pb[i % 4]
                mm(S_[:, :W], ks[0:96, jj * 128:(jj + 1) * 128], qc[0:96, hh, :W], True, True, [ks, qc], [S_])
            for i in range(min(LA, n)):
                qk(i)
            run_p1()
            for i in range(n):
                if i == P2C:
                    run_p2()
                if i + LA < n:
                    qk(i + LA)
                ch, jj = units[i]
                j = ch * KCH + jj
                vs = getv(ch)
                S_ = pb[i % 4]
                Ep = Et[ei[0] % NE]
                half = ei[1] % 2
                Ebh = Eh[ei[0] % NE][half]
                ei[1] += 1
                if half == 1:
                    ei[0] += 1
                bias_exp(S_, Ebh, Ep[:, half, :W], j, "C", hh, scaleC)
                mm(O[0:65, :W], vs[:, jj * 65:(jj + 1) * 65], Ep[:, half, :W], i == 0, i == n - 1, [vs, Ebh], [O])
            if ei[1] % 2 == 1:
                ei[1] += 1
                ei[0] += 1
            run_p2()
            pend["p1"] = (lambda hh=hh: c_fin1(hh))
            pend["p2"] = (lambda hh=hh: c_fin2(hh))
        run_p2()
        jlo = max(0, b0 - 1)
        jhi = min(NB - 1, b0 + nb)
        nj = jhi - jlo + 1
        allk = lambda kind: [db(kind, s, bq_, tt) for tt in range(ntiles)]
        P.dma(sp, kbn[:, :, 0:nj * 128], KB[s, bq_, :, :, jlo * 128:(jhi + 1) * 128].rearrange("h p n -> p h n"), r=allk("KB"), w=[kbn])
        P.dma(sp, vbn[:, 0:nj, :], VB[s, bq_, :, jlo:jhi + 1, :], r=allk("VB"), w=[vbn])
        scaleB = 64 ** -0.5
        bun = []
        for hb in range(4):
            for qi in range(nb):
                qblk = b0 + qi
                js = [j for j in (qblk - 1, qblk, qblk + 1) if 0 <= j < NB]
                for ji, j in enumerate(js):
                    bun.append((hb, qi, j, ji == 0, ji == len(js) - 1))
        nbu = len(bun)

        def bqk(u):
            hb, qi, j, first, last = bun[u]
            kvh, g = hb // 2, hb % 2
            cs = slice(qi * 128, (qi + 1) * 128)
            jl = j - jlo
            S_ = pb[u % 4]
            mm(S_[:, 0:128], kbn[g * 64:(g + 1) * 64, kvh, jl * 128:(jl + 1) * 128], qb[g * 64:(g + 1) * 64, kvh, cs], True, True, [kbn, qb], [S_])
        LB = 3
        for u in range(min(LB, nbu)):
            bqk(u)
        for u in range(nbu):
            if u + LB < nbu:
                bqk(u + LB)
            hb, qi, j, first, last = bun[u]
            kvh, g = hb // 2, hb % 2
            cs = slice(qi * 128, (qi + 1) * 128)
            jl = j - jlo
            S_ = pb[u % 4]
            O = pb[4 + hb % 2]
            dl_ = j - (b0 + qi)
            tab = TB[:, hb, (1 - dl_) * 128:(2 - dl_) * 128]
            tb = tmpb[u % 2]
            P.op(dve, lambda: nc.vector.scalar_tensor_tensor(out=tb[:, 0:128], in0=S_[:, 0:128], scalar=scaleB, in1=tab, op0=ALU.mult, op1=ALU.add),
                 r=[S_, TB], w=[tb])
            if j == 0:
                P.op(dve, lambda: nc.vector.memset(tb[0:MS, 0:128], NEG), w=[tb])
            Ep = Et[ei[0] % NE]
            Ebh = Eh[ei[0] % NE][0]
            ei[0] += 1
            P.op(act, lambda: nc.scalar.activation(out=Ep[:, 0, 0:128], in_=tb[:, 0:128], func=AF.Exp), r=[tb], w=[Ebh])
            mm(O[0:65, cs], vbn[:, jl, kvh * 65:(kvh + 1) * 65], Ep[:, 0, 0:128], first, last, [vbn, Ebh], [O])
            if qi == nb - 1 and last:
                evac(osb[0:65, :W], O[0:65, :W], [O], [osb])
                mm(pb[6 + hb % 2][0:64, :W], sel65[0:65, :], osb[0:65, :W], True, True, [sel65, osb], [pb[6 + hb % 2]])
                recip(rr[0][0:64, :W], pb[6 + hb % 2][0:64, :W], None, [pb[6 + hb % 2], esink], [rr[0]], bias=esink[0:64, 4 * l + hb:4 * l + hb + 1])
                P.op(dve, lambda: nc.vector.tensor_tensor(out=cat[0:64, 4 + hb, :W], in0=osb[0:64, :W], in1=rr[0][0:64, :W], op=ALU.mult), r=[osb, rr[0]], w=[cat])

    def post(s, t, l):
        c0, W = tile_cols(t)
        for d in range(KD):
            sl = ringD.pop()
            po = pb[4 + d % 2]
            for kc in range(12):
                if kc < 4:
                    mm(po[:, :W], sl[:, kc * 128:(kc + 1) * 128], cat[:, kc, :W], kc == 0, False, [sl, cat], [po])
                else:
                    mm(po[:, :W], sl[0:64, kc * 128:(kc + 1) * 128], cat[0:64, kc, :W], False, kc == 11, [sl, cat], [po])
            P.op(dve, lambda: nc.vector.tensor_tensor(out=h[:, d, :W], in0=po[:, :W], in1=h[:, d, :W], op=ALU.add), r=[po, hb_[d]], w=[hb_[d]])
        ffn(2 * l + 1, gc(2, l), W)

    def final(s, t):
        c0, W = tile_cols(t)
        if t == 0:
            return
        rms_scale([h[:, k, :W] for k in range(KD)], D, W, [[hb_[k]] for k in range(KD)])
        for k in range(KD):
            P.op(dve, lambda: nc.vector.scalar_tensor_tensor(out=h[:, k, :W], in0=h[:, k, :W], scalar=gcol[:, GFIN + k:GFIN + k + 1],
                                                             in1=rs[:, :W], op0=ALU.mult, op1=ALU.mult), r=[hb_[k], rs, gcol], w=[hb_[k]])
        tok0 = (t - 1) * 512
        for b in range(4):
            for half in range(2):
                pt = pb[(2 * b + half) % 4]
                for kk in range(4):
                    k = half * 4 + kk
                    P.op(pe, lambda: nc.tensor.transpose(pt[:, kk * 128:(kk + 1) * 128], h[:, k, b * 128:(b + 1) * 128], ident[:, :]), r=[hb_[k], ident], w=[pt])
                evac(big[:, b * 1024 + half * 512: b * 1024 + (half + 1) * 512], pt[:, :], [pt], [big])
        P.dma(pool, y_out[s, tok0:tok0 + 512, :].rearrange("(b p) n -> p b n", p=128), big[:, :].rearrange("p (b n) -> p b n", n=1024), r=[big], w=[db("Y", s, t)])

    prologue()
    P.barrier()
    segs = list(range(depth + 1))
    for seg in segs:
        for s in range(NSEQ):
            for t in range(ntiles):
                if seg > 0:
                    plan_attn(s, t, seg - 1)
                    plan_wout(seg - 1)
                    plan_ffn(2 * (seg - 1) + 1)
                if seg < depth:
                    plan_ffn(2 * seg)
                    plan_win(seg)
    for seg in segs:
        for s in range(NSEQ):
            for t in range(ntiles):
                if seg == 0:
                    load_x(s, t)
                else:
                    load_h(s, t)
                    attn(s, t, seg - 1)
                    post(s, t, seg - 1)
                if seg < depth:
                    pre(s, t, seg)
                    store_h(s, t)
                else:
                    final(s, t)
    P.barrier()
    es.close()
    return nc


_NC_CACHE = {}


def run(seqs, weights, S):
    NSEQ = seqs[0].shape[0]
    key = (NSEQ, S)
    if key not in _NC_CACHE:
        _NC_CACHE[key] = build(NSEQ, S)
    nc = _NC_CACHE[key]
    consts = host_consts(128 + S)
    in_maps = []
    for xs in seqs:
        m = {"x": np.ascontiguousarray(xs, dtype=np.float32)}
        m.update(weights)
        m.update(consts)
        in_maps.append(m)
    res = run_bass_kernel_spmd(nc, in_maps, core_ids=list(range(len(seqs))))
    return [r["y"] for r in res.results]


def kernel(**inputs):
    xp = np.asarray(inputs["x_prompt"], np.float32)
    xs = np.asarray(inputs["x_sample"], np.float32)
    allx = np.concatenate([xp, xs], axis=0)
    S = allx.shape[1]
    wnames = ["meta", "rel_bias", "g_ffn1", "w_ffn1_gu", "w_ffn1_down", "g_mix", "w_in", "diff_lambda", "g_subln", "sinks",
              "g_cq", "g_ckv", "w_uq", "w_ukv", "w_out", "g_ffn2", "w_ffn2_gu", "w_ffn2_down", "g_final"]
    weights = {k: np.ascontiguousarray(np.asarray(inputs[k], np.float32)) for k in wnames}
    assign = [(0, 1), (2, 3), (4, 5), (6, 7), (8, 9), (0, 1), (2, 3), (4, 5)]
    seqs = [allx[list(a)] for a in assign]
    ys = run(seqs, weights, S)
    out = np.empty_like(allx)
    for c in range(5):
        out[list(assign[c])] = ys[c]
    return (out[:xp.shape[0]].copy(), out[xp.shape[0]:].copy())
```
